# Optimizing a Trainium2 kernel written in Bass

```python
import math
import jax, jax.numpy as jnp
from jax import lax
import numpy as np

D_MODEL = 1024
BATCH = 4
SEQ = 8192
DEPTH = 4

N_A_LAYERS = DEPTH // 2
N_B_LAYERS = DEPTH - N_A_LAYERS

GDN_HEADS = 8
GDN_HEAD_DIM = D_MODEL // GDN_HEADS
GDN_WIDTH = GDN_HEADS * GDN_HEAD_DIM
CONV_K = 4
GDN_CHUNK = 64
A_IN_COLS = 4 * GDN_WIDTH + 2 * GDN_HEADS

MOBA_HEADS = 8
MOBA_HEAD_DIM = D_MODEL // MOBA_HEADS
MOBA_WIDTH = MOBA_HEADS * MOBA_HEAD_DIM
MOBA_BLOCK = 256
MOBA_TOP_K = 3
MOBA_Q_CHUNK = 32

DEEPNORM_ALPHA = (2 * DEPTH) ** 0.25
DEEPNORM_BETA = (8 * DEPTH) ** -0.25
LN_EPS = 1e-5
RMS_EPS = 1e-6

kernel_name = "yoco_gated_deltanet_moba_deepnorm"


def layer_norm(x, g, b):
    xf = x.astype(jnp.float32)
    mu = jnp.mean(xf, axis=-1, keepdims=True)
    var = jnp.mean(jnp.square(xf - mu), axis=-1, keepdims=True)
    y = (xf - mu) * lax.rsqrt(var + LN_EPS) * g.astype(jnp.float32) + b.astype(jnp.float32)
    return y.astype(x.dtype)


def l2_normalize(t):
    return t * lax.rsqrt(jnp.sum(jnp.square(t), axis=-1, keepdims=True) + RMS_EPS)


def causal_depthwise_conv(x, w):
    k_w, c = w.shape
    return lax.conv_general_dilated(
        x, w[:, None, :].astype(x.dtype), window_strides=(1,), padding=[(k_w - 1, 0)],
        dimension_numbers=("NWC", "WIO", "NWC"), feature_group_count=c)


def gated_delta_rule_chunked(q, k, v, g, beta):
    B, T, H, dk = q.shape
    dv = v.shape[-1]
    C = GDN_CHUNK
    N = T // C

    def chunks(t):
        return t.reshape(B, N, C, H, -1).transpose(1, 0, 3, 2, 4)

    q = chunks(q) * (dk ** -0.5)
    k = chunks(k)
    v = chunks(v)
    g = chunks(g[..., None])[..., 0]
    beta = chunks(beta[..., None])[..., 0]
    gc = jnp.cumsum(g, axis=-1)
    pos = jnp.arange(C)
    causal = pos[:, None] >= pos[None, :]
    strict = pos[:, None] > pos[None, :]
    decay = jnp.exp(jnp.where(causal, gc[..., :, None] - gc[..., None, :], -jnp.inf))
    k_beta = k * beta[..., None]
    lower = jnp.where(strict, jnp.einsum('nbhid,nbhjd->nbhij', k_beta, k) * decay, 0.0)
    a_mat = jnp.eye(C, dtype=jnp.float32) + lower
    w = lax.linalg.triangular_solve(a_mat, k_beta * jnp.exp(gc)[..., None],
                                    left_side=True, lower=True, unit_diagonal=True)
    u = lax.linalg.triangular_solve(a_mat, v * beta[..., None],
                                    left_side=True, lower=True, unit_diagonal=True)
    attn = jnp.einsum('nbhid,nbhjd->nbhij', q, k) * decay
    q_dec = q * jnp.exp(gc)[..., None]
    k_dec = k * jnp.exp(gc[..., -1:] - gc)[..., None]
    chunk_decay = jnp.exp(gc[..., -1])

    def step(state, inp):
        qd, wc, uc, ac, kd, cd = inp
        v_new = uc - jnp.einsum('bhcd,bhde->bhce', wc, state)
        o = jnp.einsum('bhcd,bhde->bhce', qd, state) + jnp.einsum('bhij,bhje->bhie', ac, v_new)
        state = state * cd[..., None, None] + jnp.einsum('bhcd,bhce->bhde', kd, v_new)
        return state, o

    state0 = jnp.zeros((B, H, dk, dv), jnp.float32)
    _, o = lax.scan(step, state0, (q_dec, w, u, attn, k_dec, chunk_decay))
    return o.transpose(1, 0, 3, 2, 4).reshape(B, T, H, dv)


def gated_deltanet_mixer(h, w_in, conv_w, a_log, dt_bias, norm_w, w_out):
    B, T, _ = h.shape
    proj = h @ w_in
    qkv, z, b, a = jnp.split(proj, [3 * GDN_WIDTH, 4 * GDN_WIDTH, 4 * GDN_WIDTH + GDN_HEADS], axis=-1)
    qkv = jax.nn.silu(causal_depthwise_conv(qkv, conv_w)).astype(jnp.float32)
    q, k, v = jnp.split(qkv, 3, axis=-1)
    q = l2_normalize(q.reshape(B, T, GDN_HEADS, GDN_HEAD_DIM))
    k = l2_normalize(k.reshape(B, T, GDN_HEADS, GDN_HEAD_DIM))
    v = v.reshape(B, T, GDN_HEADS, GDN_HEAD_DIM)
    beta = jax.nn.sigmoid(b.astype(jnp.float32))
    g = -jnp.exp(a_log.astype(jnp.float32)) * jax.nn.softplus(
        a.astype(jnp.float32) + dt_bias.astype(jnp.float32))
    o = gated_delta_rule_chunked(q, k, v, g, beta)
    o = o * lax.rsqrt(jnp.mean(jnp.square(o), axis=-1, keepdims=True) + RMS_EPS) * norm_w.astype(jnp.float32)
    o = o * jax.nn.silu(z.astype(jnp.float32)).reshape(B, T, GDN_HEADS, GDN_HEAD_DIM)
    return o.reshape(B, T, GDN_WIDTH).astype(h.dtype) @ w_out


def shared_kv(h, w_kv):
    B, T, _ = h.shape
    nb = -(-T // MOBA_BLOCK)
    t_pad = nb * MOBA_BLOCK
    kv = (h @ w_kv).reshape(B, T, 2, MOBA_HEADS, MOBA_HEAD_DIM)
    kv = jnp.pad(kv, ((0, 0), (0, t_pad - T), (0, 0), (0, 0), (0, 0)))
    kv = kv.transpose(2, 0, 3, 1, 4).reshape(2, B, MOBA_HEADS, nb, MOBA_BLOCK, MOBA_HEAD_DIM)
    kb, vb = kv[0], kv[1]
    kmean = jnp.mean(kb.astype(jnp.float32), axis=3).astype(kb.dtype)
    return kb, vb, kmean


def moba_attention(q, kb, vb, kmean):
    B, T, H, dh = q.shape
    nb = kb.shape[2]
    t_pad = nb * MOBA_BLOCK
    q = jnp.pad(q, ((0, 0), (0, t_pad - T), (0, 0), (0, 0))).transpose(0, 2, 1, 3) * (dh ** -0.5)
    q_blk = jnp.arange(t_pad) // MOBA_BLOCK
    gate = jnp.einsum('bhtd,bhnd->bhtn', q, kmean).astype(jnp.float32)
    gate = jnp.where(jnp.arange(nb)[None, :] < q_blk[:, None], gate, -jnp.inf)
    n_sel = min(MOBA_TOP_K, nb)
    _, sel = lax.top_k(gate, n_sel)
    n_chunks = t_pad // MOBA_Q_CHUNK
    q_c = q.reshape(B, H, n_chunks, MOBA_Q_CHUNK, dh).transpose(2, 0, 1, 3, 4)
    sel_c = sel.reshape(B, H, n_chunks, MOBA_Q_CHUNK, n_sel).transpose(2, 0, 1, 3, 4)
    bi = jnp.arange(B)[:, None, None, None]
    hi = jnp.arange(H)[None, :, None, None]

    def one_chunk(args):
        c, qc, sc = args
        q_pos = c * MOBA_Q_CHUNK + jnp.arange(MOBA_Q_CHUNK)
        own = (c * MOBA_Q_CHUNK) // MOBA_BLOCK
        k_sel = kb[bi, hi, sc]
        v_sel = vb[bi, hi, sc]
        s_sel = jnp.einsum('bhqd,bhqskd->bhqsk', qc, k_sel).astype(jnp.float32)
        valid = jnp.arange(n_sel)[None, :] < (q_pos // MOBA_BLOCK)[:, None]
        s_sel = jnp.where(valid[None, None, :, :, None], s_sel, -jnp.inf)
        s_sel = s_sel.reshape(B, H, MOBA_Q_CHUNK, n_sel * MOBA_BLOCK)
        k_own = lax.dynamic_index_in_dim(kb, own, axis=2, keepdims=False)
        v_own = lax.dynamic_index_in_dim(vb, own, axis=2, keepdims=False)
        s_own = jnp.einsum('bhqd,bhkd->bhqk', qc, k_own).astype(jnp.float32)
        k_pos = own * MOBA_BLOCK + jnp.arange(MOBA_BLOCK)
        s_own = jnp.where(k_pos[None, :] <= q_pos[:, None], s_own, -jnp.inf)
        p = jax.nn.softmax(jnp.concatenate([s_sel, s_own], axis=-1), axis=-1)
        p_sel = p[..., :n_sel * MOBA_BLOCK].reshape(B, H, MOBA_Q_CHUNK, n_sel, MOBA_BLOCK).astype(vb.dtype)
        p_own = p[..., n_sel * MOBA_BLOCK:].astype(vb.dtype)
        return (jnp.einsum('bhqsk,bhqskd->bhqd', p_sel, v_sel)
                + jnp.einsum('bhqk,bhkd->bhqd', p_own, v_own))

    o = lax.map(one_chunk, (jnp.arange(n_chunks), q_c, sel_c))
    o = o.transpose(1, 0, 3, 2, 4).reshape(B, t_pad, H, dh)
    return o[:, :T]


def moba_mixer(h, w_in, w_out, kb, vb, kmean):
    B, T, _ = h.shape
    q, z = jnp.split(h @ w_in, 2, axis=-1)
    o = moba_attention(q.reshape(B, T, MOBA_HEADS, MOBA_HEAD_DIM), kb, vb, kmean)
    o = o.reshape(B, T, MOBA_WIDTH) * jax.nn.silu(z)
    return o @ w_out


def setup_inputs(seed: int = 0) -> dict:
    key = jax.random.key(seed)
    ks = jax.random.split(key, 16)
    nrm = jax.random.normal
    f32 = jnp.float32
    x = nrm(ks[0], (BATCH, SEQ, D_MODEL), f32)
    a_w_in = nrm(ks[1], (N_A_LAYERS, D_MODEL, A_IN_COLS), f32) * D_MODEL ** -0.5
    a_conv_w = nrm(ks[2], (N_A_LAYERS, CONV_K, 3 * GDN_WIDTH), f32) * CONV_K ** -0.5
    a_A_log = jnp.log(jax.random.uniform(ks[3], (N_A_LAYERS, GDN_HEADS), f32, 1.0, 16.0))
    dt = jnp.exp(jax.random.uniform(ks[4], (N_A_LAYERS, GDN_HEADS), f32,
                                    math.log(1e-3), math.log(1e-1)))
    a_dt_bias = jnp.log(jnp.expm1(dt))
    a_norm_w = 1.0 + 0.02 * nrm(ks[5], (N_A_LAYERS, GDN_HEAD_DIM), f32)
    a_w_out = nrm(ks[6], (N_A_LAYERS, GDN_WIDTH, D_MODEL), f32) * GDN_WIDTH ** -0.5 * DEEPNORM_BETA
    a_ln_g = 1.0 + 0.02 * nrm(ks[7], (N_A_LAYERS, D_MODEL), f32)
    a_ln_b = 0.02 * nrm(ks[8], (N_A_LAYERS, D_MODEL), f32)
    b_w_kv = nrm(ks[9], (D_MODEL, 2 * MOBA_WIDTH), f32) * D_MODEL ** -0.5
    b_w_in = nrm(ks[10], (N_B_LAYERS, D_MODEL, 2 * MOBA_WIDTH), f32) * D_MODEL ** -0.5
    b_w_out = nrm(ks[11], (N_B_LAYERS, MOBA_WIDTH, D_MODEL), f32) * MOBA_WIDTH ** -0.5 * DEEPNORM_BETA
    b_ln_g = 1.0 + 0.02 * nrm(ks[12], (N_B_LAYERS, D_MODEL), f32)
    b_ln_b = 0.02 * nrm(ks[13], (N_B_LAYERS, D_MODEL), f32)
    return {"x": x, "a_w_in": a_w_in, "a_conv_w": a_conv_w, "a_A_log": a_A_log,
            "a_dt_bias": a_dt_bias, "a_norm_w": a_norm_w, "a_w_out": a_w_out,
            "a_ln_g": a_ln_g, "a_ln_b": a_ln_b, "b_w_kv": b_w_kv, "b_w_in": b_w_in,
            "b_w_out": b_w_out, "b_ln_g": b_ln_g, "b_ln_b": b_ln_b}


def reference(x, a_w_in, a_conv_w, a_A_log, a_dt_bias, a_norm_w, a_w_out, a_ln_g, a_ln_b,
              b_w_kv, b_w_in, b_w_out, b_ln_g, b_ln_b):
    kb = vb = kmean = None
    for layer in range(DEPTH):
        if layer < N_A_LAYERS:
            i = layer
            y = gated_deltanet_mixer(x, a_w_in[i], a_conv_w[i], a_A_log[i], a_dt_bias[i],
                                     a_norm_w[i], a_w_out[i])
            x = layer_norm(DEEPNORM_ALPHA * x + y, a_ln_g[i], a_ln_b[i])
            if layer == N_A_LAYERS - 1:
                kb, vb, kmean = shared_kv(x, b_w_kv)
        else:
            i = layer - N_A_LAYERS
            y = moba_mixer(x, b_w_in[i], b_w_out[i], kb, vb, kmean)
            x = layer_norm(DEEPNORM_ALPHA * x + y, b_ln_g[i], b_ln_b[i])
    return x
```

```python
import numpy as np
from contextlib import ExitStack
import concourse.bass as bass
import concourse.mybir as mybir

F32 = mybir.dt.float32
BF16 = mybir.dt.bfloat16
ALU = mybir.AluOpType
AF = mybir.ActivationFunctionType
AX = mybir.AxisListType


class Sched:
    def __init__(self, nc, es, ndma=8):
        self.nc = nc
        self.es = es
        self.E = {'pe': nc.tensor, 'dve': nc.vector, 'act': nc.scalar, 'pool': nc.gpsimd, 'sp': nc.sync}
        self.csem = {e: es.enter_context(nc.semaphore("c_" + e)) for e in ('pe', 'dve', 'act', 'pool')}
        self.cnt = {e: 0 for e in self.csem}
        self.ndma = ndma
        self.dsem = {q: [es.enter_context(nc.semaphore("d_%s%d" % (q, i))) for i in range(ndma)]
                     for q in ('sp', 'pool', 'act')}
        self.dcnt = {q: 0 for q in self.dsem}
        self.seen = {e: {} for e in self.E}
        self.lastw = {}
        self.readers = {}
        self.nwaits = 0

    def _sem(self, key):
        return self.csem[key[1]] if key[0] == 'c' else self.dsem[key[1]][key[2]]

    def _wait(self, e, tick):
        key, val = tick
        if self.seen[e].get(key, 0) >= val:
            return
        self.E[e].wait_ge(self._sem(key), val)
        self.seen[e][key] = val
        self.nwaits += 1

    def _deps(self, e, r, w):
        for k in r:
            t = self.lastw.get(k)
            if t is not None:
                self._wait(e, t)
        for k in w:
            t = self.lastw.get(k)
            if t is not None and (t[0] != ('c', e) or e != 'pe'):
                self._wait(e, t)
            for t in self.readers.get(k, {}).values():
                if t[0] != ('c', e) or e != 'pe':
                    self._wait(e, t)

    def _mark(self, tick, r, w):
        for k in r:
            self.readers.setdefault(k, {})[tick[0]] = tick
        for k in w:
            self.lastw[k] = tick
            self.readers[k] = {}

    @staticmethod
    def is_psum(k):
        while isinstance(k, tuple):
            k = k[0]
        return k.startswith('p')

    def op(self, e, fn, r=(), w=(), sig=True):
        w = list(w) + [k for k in r if self.is_psum(k)]
        r = [k for k in r if not self.is_psum(k)]
        self._deps(e, r, w)
        ins = fn(self.E[e])
        if sig:
            self.cnt[e] += 1
            ins.then_inc(self.csem[e], 1)
            tick = (('c', e), self.cnt[e])
        else:
            tick = (('c', e), self.cnt[e] + 1)
        self._mark(tick, r, w)
        return ins

    def dma(self, q, out, in_, r=(), w=(), **kw):
        self._deps(q, r, w)
        n = self.dcnt[q]
        slot = n % self.ndma
        key = ('d', q, slot)
        rnd = n // self.ndma
        if rnd > 0:
            self._wait(q, (key, 16 * rnd))
        self.E[q].dma_start(out=out, in_=in_, **kw).then_inc(self.dsem[q][slot], 16)
        self.dcnt[q] = n + 1
        self._mark((key, 16 * (rnd + 1)), r, w)

    def barrier(self):
        for e in self.E:
            for e2 in self.csem:
                if e2 != e and self.cnt[e2] > 0:
                    self._wait(e, (('c', e2), self.cnt[e2]))
            for q in self.dsem:
                n = self.dcnt[q]
                for slot in range(min(n, self.ndma)):
                    cnt = (n - slot + self.ndma - 1) // self.ndma
                    self._wait(e, (('d', q, slot), 16 * cnt))

    def finish(self):
        for q in self.dsem:
            n = self.dcnt[q]
            for slot in range(min(n, self.ndma)):
                last = ((n - 1 - slot) // self.ndma) + 1 if n - 1 >= slot else 0
                cnt = (n - slot + self.ndma - 1) // self.ndma
                self._wait(q, (('d', q, slot), 16 * cnt))


T = 8192
D = 1024
TB = 512
CH = 128
NLEV = 6
RMS_EPS = 1e-6


def setup_consts(nc, S, es):
    sb = lambda name, shape, dt: es.enter_context(nc.sbuf_tensor(name, shape, dt))
    C = {}
    ones = sb("c_ones", [128, 512], F32)
    idf = sb("c_idf", [128, 128], F32)
    idb = sb("c_idb", [128, 128], BF16)
    mincl = sb("c_mincl", [128, 128], F32)
    mstrict = sb("c_mstrict", [128, 128], F32)
    rmask = sb("c_rmask", [4, 4, 128], F32)
    esel = sb("c_esel", [4, 4, 128], F32)
    onesb = sb("c_onesb", [128, 128], BF16)
    S.op('pool', lambda e: e.memset(ones[:], 1.0), w=['c_ones'])
    S.op('pool', lambda e: e.memset(onesb[:], 1.0), w=['c_onesb'])
    S.op('pool', lambda e: e.affine_select(idf[:], ones[:, 0:128], [[-1, 128]], ALU.is_equal, 0.0, base=0, channel_multiplier=1), r=['c_ones'], w=['c_idf'])
    S.op('pool', lambda e: e.tensor_copy(idb[:], idf[:]), r=['c_idf'], w=['c_idb'])
    S.op('pool', lambda e: e.affine_select(mincl[:], ones[:, 0:128], [[1, 128]], ALU.is_ge, 0.0, base=0, channel_multiplier=-1), r=['c_ones'], w=['c_mincl'])
    S.op('pool', lambda e: e.affine_select(mstrict[:], ones[:, 0:128], [[1, 128]], ALU.is_gt, 0.0, base=0, channel_multiplier=-1), r=['c_ones'], w=['c_mstrict'])
    S.op('pool', lambda e: e.affine_select(rmask[:], ones[0:4, 0:512].rearrange("p (a b) -> p a b", b=128), [[0, 4], [1, 128]], ALU.not_equal, 0.0, base=0, channel_multiplier=0), r=['c_ones'], w=['c_rmask'])
    S.op('pool', lambda e: e.affine_select(esel[:], ones[0:4, 0:512].rearrange("p (a b) -> p a b", b=128), [[-1, 4], [0, 128]], ALU.is_equal, 0.0, base=0, channel_multiplier=1), r=['c_ones'], w=['c_esel'])
    C.update(ones=ones, idf=idf, idb=idb, mincl=mincl, mstrict=mstrict, rmask=rmask, esel=esel, onesb=onesb)
    return C


def load_cast_weight(nc, S, stage, wdram, wsb, ncols, name, engs=('pool', 'dve', 'act')):
    wv = wdram.rearrange("(kc p) n -> p kc n", p=128)
    CW = 768
    i = 0
    for kc in range(8):
        for c0 in range(0, ncols, CW):
            c1 = min(ncols, c0 + CW)
            st = stage[i % 2]
            S.dma('sp', st[:, 0:c1 - c0], wv[:, kc, c0:c1], w=[('g_qkvT', 2 * (i % 2)), ('g_qkvT', 2 * (i % 2) + 1)])
            eng = engs[i % len(engs)]
            if eng == 'act':
                S.op('act', lambda e, st=st, kc=kc, c0=c0, c1=c1: e.copy(wsb[:, kc, c0:c1], st[:, 0:c1 - c0]), r=[('g_qkvT', 2 * (i % 2)), ('g_qkvT', 2 * (i % 2) + 1)], w=[(name, kc)])
            else:
                S.op(eng, lambda e, st=st, kc=kc, c0=c0, c1=c1: e.tensor_copy(wsb[:, kc, c0:c1], st[:, 0:c1 - c0]), r=[('g_qkvT', 2 * (i % 2)), ('g_qkvT', 2 * (i % 2) + 1)], w=[(name, kc)])
            i += 1


def load_xT(nc, S, C, xdram_blk, xt, xT, pbig, tag, nsub=4, F=1024):
    S.dma('sp', xt[:, 0:nsub, :], xdram_blk.rearrange("(s p) f -> p s f", p=128), w=[tag + '_xt'])
    nk = F // 128
    for kc in range(nk):
        pb = pbig[kc % 2]
        for s in range(nsub):
            S.op('pe', lambda e, pb=pb, s=s, kc=kc: e.transpose(pb[:, s * 128:(s + 1) * 128], xt[:, s, kc * 128:(kc + 1) * 128], C['idf'][:]),
                 r=[tag + '_xt', 'c_idf'], w=[('pbig', kc % 2)], sig=(s == nsub - 1))
        if kc % 2 == 0:
            S.op('act', lambda e, pb=pb, kc=kc: e.copy(xT[:, kc, 0:nsub * 128], pb[:, 0:nsub * 128]), r=[('pbig', kc % 2)], w=[(tag + '_xT', kc)])
        else:
            S.op('dve', lambda e, pb=pb, kc=kc: e.tensor_copy(xT[:, kc, 0:nsub * 128], pb[:, 0:nsub * 128]), r=[('pbig', kc % 2)], w=[(tag + '_xT', kc)])


class _Stop(Exception):
    pass


def gdn_phase(*a, **k):
    try:
        _gdn_phase(*a, **k)
    except _Stop:
        pass


def _gdn_phase(nc, S, es, C, xin, wqkv, wz, wab, convw, hp, normw, og, nblk=T // TB):
    import os
    STOP = int(os.environ.get('STOP_AT', '99'))
    def stage(n):
        if STOP == n:
            raise _Stop()
    sb = lambda name, shape, dt: es.enter_context(nc.sbuf_tensor(name, shape, dt))
    ps = lambda name, shape, dt: es.enter_context(nc.psum_tensor(name, shape, dt))
    idf, idb = C['idf'], C['idb']
    wqkv_b = sb("g_wqkv", [128, 8, 1536], BF16)
    wz_b = sb("g_wz", [128, 8, 512], BF16)
    wab_b = sb("g_wab", [128, 8, 8], BF16)
    convw_s = sb("g_convw", [128, 12, 4], F32)
    normw_s = sb("g_normw", [128, 512], F32)
    hp_s = sb("g_hp", [4, 2], F32)
    negA = sb("g_negA", [4, 1], F32)
    xt = sb("g_xt", [128, 4, 1024], F32)
    xT = sb("g_xT", [128, 8, 512], BF16)
    pc = sb("g_pc", [128, 12, 3 + TB], F32)
    cv = [sb("g_cv%d" % i, [128, TB], F32) for i in range(2)]
    qkvT = sb("g_qkvT", [128, 12, TB], F32)
    qflat = qkvT[:].rearrange("p a b -> p (a b)")
    wstage = [qflat[:, 0:768], qflat[:, 1024:1792]]
    sq = cv[1]
    zs = cv[0]
    rn = sb("g_rn", [128, TB], F32)
    qT_b = sb("g_qTb", [128, 4, TB], BF16)
    kT_b = sb("g_kTb", [128, 4, TB], BF16)
    rows = {n: sb("g_row_" + n, [4, TB], F32) for n in ('ax', 'relu', 'gc', 'ngc', 'beta', 'kbe', 'kdec')}
    for a_ in ('ex', 'ln', 'g'):
        rows[a_] = rows['ax']
    rows['egc'] = rows['kbe']
    gc_bc = sb("g_gcbc", [128, 4, TB], F32)
    egc_bc = sb("g_egcbc", [128, 4, TB], F32)
    beta_bc = sb("g_betabc", [128, 4, TB], F32)
    cols = sb("g_cols", [128, 4, 4, 4], F32)
    kbe = sb("g_kbe", [128, 4, 4, 128], BF16)
    kdec = sb("g_kdec", [128, 4, 4, 128], BF16)
    bv = sb("g_bv", [128, 4, 4, 128], BF16)
    zg = sb("g_zg", [128, 4, 512], F32)
    S32 = sb("g_S32", [128, 4, 128], F32)
    S16 = sb("g_S16", [128, 4, 128], BF16)
    NP = 2
    arg = [sb("g_arg%d" % i, [128, 128], F32) for i in range(NP)]
    E1 = [sb("g_E1%d" % i, [128, 128], F32) for i in range(NP)]
    G = [sb("g_G%d" % i, [128, 128], F32) for i in range(NP)]
    Gs = [sb("g_Gs%d" % i, [128, 128], F32) for i in range(NP)]
    Gb = [sb("g_Gb%d" % i, [128, 128], F32) for i in range(NP)]
    XY = [[sb("g_XY%d_%d" % (i, j), [128, 256], F32) for j in range(2)] for i in range(NP)]
    Mt = [[sb("g_M%d_%d" % (i, j), [128, 128], F32) for j in range(2)] for i in range(NP)]
    AiT = [sb("g_AiT%d" % i, [128, 128], BF16) for i in range(NP)]
    attnT = [sb("g_attnT%d" % i, [128, 128], BF16) for i in range(NP)]
    nwT = [sb("g_nwT%d" % i, [128, 128], BF16) for i in range(NP)]
    qdT = [sb("g_qdT%d" % i, [128, 128], BF16) for i in range(NP)]
    vn_b = [sb("g_vnb%d" % i, [128, 128], BF16) for i in range(NP)]
    junk = [sb("g_junk%d" % i, [128, 128], F32) for i in range(NP)]
    ss = [sb("g_ss%d" % i, [128, 4], F32) for i in range(NP)]
    ogt = [sb("g_ogt%d" % i, [128, 512], BF16) for i in range(2)]
    pbig = [ps("g_pbig%d" % i, [128, 512], F32) for i in range(2)]
    psm = ps("g_psm", [128, 512], F32)
    pkq = ps("g_pkq", [128, 512], F32)
    pinv = [ps("g_pinv%d" % i, [128, 512], F32) for i in range(2)]
    pscan = [ps("g_pscan%d" % i, [128, 512], F32) for i in range(2)]
    ptb = pscan
    load_cast_weight(nc, S, wstage, wqkv, wqkv_b, 1536, 'g_wqkv')
    load_cast_weight(nc, S, wstage, wz, wz_b, 512, 'g_wz')
    load_cast_weight(nc, S, wstage, wab, wab_b, 8, 'g_wab')
    S.dma('sp', convw_s[:], convw, w=['g_convw'])
    S.dma('sp', normw_s[:], normw, w=['g_normw'])
    S.dma('sp', hp_s[:], hp, w=['g_hp'])
    S.op('act', lambda e: e.activation(negA[:], hp_s[:, 1:2], AF.Exp), r=['g_hp'], w=['g_negA'])
    S.op('dve', lambda e: e.tensor_scalar(negA[:], negA[:], -1.0, None, ALU.mult), r=['g_negA'], w=['g_negA'])
    S.op('pool', lambda e: e.memset(pc[:, :, 0:3], 0.0), w=['g_pc_halo'])
    S.op('pool', lambda e: e.memset(S32[:], 0.0), w=[('g_S32', h) for h in range(4)])
    S.op('pool', lambda e: e.memset(S16[:], 0.0), w=[('g_S16', h) for h in range(4)])
    WQ = [('g_wqkv', kc) for kc in range(8)]
    WZ = [('g_wz', kc) for kc in range(8)]
    WAB = [('g_wab', kc) for kc in range(8)]
    XT = [('g_xT', kc) for kc in range(8)]

    stage(0)
    it = 0
    for blk in range(nblk):
        t0 = blk * TB
        load_xT(nc, S, C, xin[t0:t0 + TB, :], xt, xT, pbig, 'g')
        stage(1)
        for half in (0, 1):
            for kc in range(8):
                S.op('pe', lambda e, kc=kc, half=half: e.matmul(pbig[half][0:4, 0:TB], wab_b[:, kc, half * 4:half * 4 + 4], xT[:, kc, :], start=(kc == 0), stop=(kc == 7)),
                     r=[('g_wab', kc), ('g_xT', kc)], w=[('pbig', half)], sig=(kc == 7))
        R = rows
        S.op('act', lambda e: e.activation(R['beta'][:], pbig[1][0:4, 0:TB], AF.Sigmoid), r=[('pbig', 1)], w=['r_beta'])
        S.op('act', lambda e: e.activation(R['ax'][:], pbig[0][0:4, 0:TB], AF.Abs, bias=hp_s[:, 0:1]), r=[('pbig', 0), 'g_hp'], w=['r_ax'])
        S.op('act', lambda e: e.activation(R['relu'][:], pbig[0][0:4, 0:TB], AF.Relu, bias=hp_s[:, 0:1]), r=[('pbig', 0), 'g_hp'], w=['r_relu'])
        S.op('act', lambda e: e.activation(R['ex'][:], R['ax'][:], AF.Exp, scale=-1.0), r=['r_ax'], w=['r_ax'])
        S.op('act', lambda e: e.activation(R['ln'][:], R['ex'][:], AF.Ln, bias=1.0), r=['r_ax'], w=['r_ax'])
        S.op('dve', lambda e: e.tensor_tensor(R['g'][:], R['ln'][:], R['relu'][:], ALU.add), r=['r_ax', 'r_relu'], w=['r_ax'])
        S.op('dve', lambda e: e.tensor_scalar(R['g'][:], R['g'][:], negA[:, 0:1], None, ALU.mult), r=['r_ax', 'g_negA'], w=['r_ax'])
        S.op('dve', lambda e: e.tensor_tensor_scan(R['gc'][:], C['rmask'][:].rearrange("p a b -> p (a b)"), R['g'][:], 0.0, ALU.mult, ALU.add), r=['r_ax', 'c_rmask'], w=['r_gc'])
        S.op('dve', lambda e: e.tensor_scalar(R['ngc'][:], R['gc'][:], -1.0, None, ALU.mult), r=['r_gc'], w=['r_ngc'])
        S.op('act', lambda e: e.activation(R['egc'][:], R['gc'][:], AF.Exp), r=['r_gc'], w=['r_kbe'])
        S.op('dve', lambda e: e.tensor_tensor(R['kbe'][:], R['egc'][:], R['beta'][:], ALU.mult), r=['r_kbe', 'r_beta'], w=['r_kbe'])
        for c in range(4):
            cc = slice(c * CH, (c + 1) * CH)
            S.op('act', lambda e, cc=cc, c=c: e.activation(R['kdec'][:, cc], R['gc'][:, cc], AF.Exp, bias=R['gc'][:, c * CH + CH - 1:c * CH + CH], scale=-1.0),
                 r=['r_gc'], w=[('r_kdec', c)])
        stage(2)
        for qi, qn in enumerate(('ngc', 'kbe', 'kdec', 'beta')):
            for c in range(4):
                S.op('pe', lambda e, qi=qi, qn=qn, c=c: e.transpose(psm[:, (qi * 4 + c) * 4:(qi * 4 + c) * 4 + 4], R[qn][:, c * CH:(c + 1) * CH], idf[0:4, 0:4]),
                     r=([('r_kdec', c)] if qn == 'kdec' else ['r_' + qn]) + ['c_idf'], w=['ps_cols'], sig=(qi == 3 and c == 3))
        S.op('dve', lambda e: e.tensor_copy(cols[:].rearrange("p a b c -> p (a b c)"), psm[:, 0:64]), r=['ps_cols'], w=['g_cols'])
        stage(3)
        for h in range(4):
            S.op('pe', lambda e, h=h: e.matmul(pbig[0][:, :], C['esel'][:, h, :], R['gc'][:], start=True, stop=True), r=['r_gc', 'c_esel'], w=[('pbig', 0)])
            S.op('act', lambda e, h=h: e.copy(gc_bc[:, h, :], pbig[0][:, :]), r=[('pbig', 0)], w=[('g_gcbc', h)])
            S.op('act', lambda e, h=h: e.activation(egc_bc[:, h, :], pbig[0][:, :], AF.Exp), r=[('pbig', 0)], w=[('g_egcbc', h)])
            S.op('pe', lambda e, h=h: e.matmul(pbig[1][:, :], C['esel'][:, h, :], R['beta'][:], start=True, stop=True), r=['r_beta', 'c_esel'], w=[('pbig', 1)])
            S.op('dve', lambda e, h=h: e.tensor_copy(beta_bc[:, h, :], pbig[1][:, :]), r=[('pbig', 1)], w=[('g_betabc', h)])
        stage(4)
        for ct in range(12):
            pb = pbig[ct % 2]
            for kc in range(8):
                S.op('pe', lambda e, pb=pb, ct=ct, kc=kc: e.matmul(pb[:, :], wqkv_b[:, kc, ct * 128:(ct + 1) * 128], xT[:, kc, :], start=(kc == 0), stop=(kc == 7)),
                     r=[('g_wqkv', kc), ('g_xT', kc)], w=[('pbig', ct % 2)], sig=(kc == 7))
            S.op('act', lambda e, pb=pb, ct=ct: e.copy(pc[:, ct, 3:3 + TB], pb[:, :]), r=[('pbig', ct % 2)], w=[('g_pc', ct)])
            a, b = cv[0], cv[1]
            rr = [('g_pc', ct), 'g_pc_halo', 'g_convw']
            S.op('dve', lambda e, ct=ct: e.tensor_scalar(a[:], pc[:, ct, 0:TB], convw_s[:, ct, 0:1], None, ALU.mult), r=rr, w=['g_cv0'])
            S.op('dve', lambda e, ct=ct: e.scalar_tensor_tensor(b[:], pc[:, ct, 1:1 + TB], convw_s[:, ct, 1:2], a[:], ALU.mult, ALU.add), r=rr + ['g_cv0'], w=['g_cv1'])
            S.op('dve', lambda e, ct=ct: e.scalar_tensor_tensor(a[:], pc[:, ct, 2:2 + TB], convw_s[:, ct, 2:3], b[:], ALU.mult, ALU.add), r=rr + ['g_cv1'], w=['g_cv0'])
            S.op('dve', lambda e, ct=ct: e.scalar_tensor_tensor(b[:], pc[:, ct, 3:3 + TB], convw_s[:, ct, 3:4], a[:], ALU.mult, ALU.add), r=rr + ['g_cv0'], w=['g_cv1'])
            S.op('act', lambda e, ct=ct: e.activation(qkvT[:, ct, :], b[:], AF.Silu), r=['g_cv1'], w=[('g_qkvT', ct)])
        S.op('pool', lambda e: e.tensor_copy(pc[:, :, 0:3], pc[:, :, TB:TB + 3]), r=[('g_pc', ct) for ct in range(12)], w=['g_pc_halo'])
        stage(5)
        for ct in range(8):
            pb = pbig[ct % 2]
            S.op('pool', lambda e, ct=ct: e.tensor_tensor(sq[:], qkvT[:, ct, :], qkvT[:, ct, :], ALU.mult), r=[('g_qkvT', ct)], w=['g_cv1'])
            S.op('pe', lambda e, pb=pb: e.matmul(pb[:, :], C['ones'][:, 0:128], sq[:], start=True, stop=True), r=['g_cv1', 'c_ones'], w=[('pbig', ct % 2)])
            S.op('act', lambda e, pb=pb: e.activation(rn[:], pb[:, :], AF.Sqrt, bias=RMS_EPS), r=[('pbig', ct % 2)], w=['g_rn'])
            S.op('dve', lambda e: e.reciprocal(rn[:], rn[:]), r=['g_rn'], w=['g_rn'])
            dst = qT_b if ct < 4 else kT_b
            scl = (128.0 ** -0.5) if ct < 4 else 1.0
            S.op('dve', lambda e, ct=ct, dst=dst, scl=scl: e.scalar_tensor_tensor(dst[:, ct % 4, :], qkvT[:, ct, :], scl, rn[:], ALU.mult, ALU.mult),
                 r=[('g_qkvT', ct), 'g_rn'], w=[('g_qkT', ct)])
        stage(6)
        for c in range(4):
            cc = slice(c * CH, (c + 1) * CH)
            for h in range(4):
                pb = pbig[h % 2]
                S.op('pool', lambda e, h=h, cc=cc: e.tensor_copy(sq[:, 0:128], kT_b[:, h, cc]), r=[('g_qkT', 4 + h)], w=['g_cv1'])
                S.op('pe', lambda e, pb=pb: e.transpose(pb[:, 0:128], sq[:, 0:128], idf[:]), r=['g_cv1', 'c_idf'], w=[('pbig', h % 2)], sig=False)
                S.op('pe', lambda e, pb=pb, h=h, cc=cc: e.transpose(pb[:, 128:256], qkvT[:, 8 + h, cc], idf[:]), r=[('g_qkvT', 8 + h), 'c_idf'], w=[('pbig', h % 2)])
                S.op('dve', lambda e, pb=pb, c=c, h=h: e.tensor_scalar(kbe[:, c, h, :], pb[:, 0:128], cols[:, 1, c, h:h + 1], None, ALU.mult), r=[('pbig', h % 2), 'g_cols'], w=[('g_kbe', c, h)])
                S.op('dve', lambda e, pb=pb, c=c, h=h: e.tensor_scalar(kdec[:, c, h, :], pb[:, 0:128], cols[:, 2, c, h:h + 1], None, ALU.mult), r=[('pbig', h % 2), 'g_cols'], w=[('g_kdec', c, h)])
                S.op('dve', lambda e, pb=pb, c=c, h=h: e.tensor_scalar(bv[:, c, h, :], pb[:, 128:256], cols[:, 3, c, h:h + 1], None, ALU.mult), r=[('pbig', h % 2), 'g_cols'], w=[('g_bv', c, h)])
            pb = pbig[c % 2]
            for kc in range(8):
                S.op('pe', lambda e, pb=pb, kc=kc, cc=cc: e.matmul(pb[:, :], xT[:, kc, cc], wz_b[:, kc, :], start=(kc == 0), stop=(kc == 7)),
                     r=[('g_wz', kc), ('g_xT', kc)], w=[('pbig', c % 2)], sig=(kc == 7))
            S.op('act', lambda e, pb=pb: e.activation(zs[:], pb[:, :], AF.Silu), r=[('pbig', c % 2)], w=['g_cv0'])
            S.op('pool', lambda e, c=c: e.tensor_tensor(zg[:, c, :], zs[:], normw_s[:], ALU.mult), r=['g_cv0', 'g_normw'], w=[('g_zg', c)])
        stage(7)
        for c in range(4):
            cc = slice(c * CH, (c + 1) * CH)
            og_t = ogt[c % 2]
            for h in range(4):
                p = it % NP
                it += 1
                k_ = lambda s: (s, p)
                S.op('dve', lambda e, h=h, cc=cc, c=c: e.tensor_scalar(arg[p][:], gc_bc[:, h, cc], cols[:, 0, c, h:h + 1], 0.0, ALU.add, ALU.min),
                     r=[('g_gcbc', h), 'g_cols'], w=[k_('arg')])
                S.op('act', lambda e: e.activation(E1[p][:], arg[p][:], AF.Exp), r=[k_('arg')], w=[k_('E1')])
                S.op('pool', lambda e: e.tensor_tensor(G[p][:], E1[p][:], C['mincl'][:], ALU.mult), r=[k_('E1'), 'c_mincl'], w=[k_('G')])
                S.op('pool', lambda e: e.tensor_tensor(Gs[p][:], E1[p][:], C['mstrict'][:], ALU.mult), r=[k_('E1'), 'c_mstrict'], w=[k_('Gs')])
                S.op('pool', lambda e, h=h, cc=cc: e.tensor_tensor(Gb[p][:], Gs[p][:], beta_bc[:, h, cc], ALU.mult), r=[k_('Gs'), ('g_betabc', h)], w=[k_('Gb')])
                S.op('pe', lambda e, h=h, cc=cc: e.matmul(pkq[:, 0:128], kT_b[:, h, cc], kT_b[:, h, cc], start=True, stop=True), r=[('g_qkT', 4 + h)], w=['pkq'], sig=False)
                S.op('pe', lambda e, h=h, cc=cc: e.matmul(pkq[:, 128:256], kT_b[:, h, cc], qT_b[:, h, cc], start=True, stop=True), r=[('g_qkT', 4 + h), ('g_qkT', h)], w=['pkq'])
                X0 = XY[p][0]
                S.op('dve', lambda e: e.tensor_tensor(X0[:, 0:128], pkq[:, 0:128], Gb[p][:], ALU.mult), r=['pkq', k_('Gb')], w=[k_('XY0')])
                S.op('dve', lambda e: e.tensor_tensor(attnT[p][:], pkq[:, 128:256], G[p][:], ALU.mult), r=['pkq', k_('G')], w=[k_('attnT')])
                S.op('pool', lambda e, h=h, cc=cc: e.tensor_tensor(qdT[p][:], qT_b[:, h, cc], egc_bc[:, h, cc], ALU.mult), r=[('g_qkT', h), ('g_egcbc', h)], w=[k_('qdT')])
                stage(8)
                pi = pinv[p]
                S.op('pe', lambda e: e.transpose(pi[:, 128:256], X0[:, 0:128], idf[:]), r=[k_('XY0'), 'c_idf'], w=[k_('pinv')])
                S.op('act', lambda e: e.copy(X0[:, 128:256], pi[:, 128:256]), r=[k_('pinv')], w=[k_('XY0y')])
                S.op('dve', lambda e: e.tensor_tensor(Mt[p][0][:], idf[:], X0[:, 0:128], ALU.subtract), r=[k_('XY0'), 'c_idf'], w=[k_('M0')])
                stage(85)
                for lv in range(1, NLEV + 2):
                    stage(85 + lv)
                    src = XY[p][(lv - 1) % 2]
                    dst = XY[p][lv % 2]
                    msrc = Mt[p][lv % 2]
                    mdst = Mt[p][(lv + 1) % 2]
                    sk = [k_('XY%d' % ((lv - 1) % 2)), k_('XY%dy' % ((lv - 1) % 2))]
                    mk = k_('M%d' % (lv % 2))
                    pe_ops = []
                    if lv <= NLEV:
                        pe_ops.append((lambda e, src=src: e.matmul(pi[:, 0:128], src[:, 128:256], src[:, 0:128], start=True, stop=True), sk))
                        pe_ops.append((lambda e, src=src: e.matmul(pi[:, 128:256], src[:, 0:128], src[:, 128:256], start=True, stop=True), sk))
                    if lv >= 2:
                        pe_ops.append((lambda e, msrc=msrc, src=src: e.matmul(pi[:, 256:384], src[:, 128:256], msrc[:], start=True, stop=True), sk + [mk]))
                    for i_, (fn_, rr_) in enumerate(pe_ops):
                        S.op('pe', fn_, r=rr_, w=[k_('pinv')], sig=(i_ == len(pe_ops) - 1))
                    if lv <= NLEV:
                        wk = [k_('XY%d' % (lv % 2)), k_('XY%dy' % (lv % 2))]
                        S.op('dve', lambda e, dst=dst: e.tensor_copy(dst[:, 0:256], pi[:, 0:256]), r=[k_('pinv')], w=wk)
                    if lv >= 2:
                        if lv <= NLEV:
                            S.op('dve', lambda e, mdst=mdst, msrc=msrc: e.tensor_tensor(mdst[:], pi[:, 256:384], msrc[:], ALU.add), r=[k_('pinv'), mk], w=[k_('M%d' % ((lv + 1) % 2))])
                        else:
                            S.op('dve', lambda e, msrc=msrc: e.tensor_tensor(AiT[p][:], pi[:, 256:384], msrc[:], ALU.add), r=[k_('pinv'), mk], w=[k_('AiT')])
                stage(9)
                S.op('pe', lambda e, c=c, h=h: e.matmul(pkq[:, 256:384], kbe[:, c, h, :], AiT[p][:], start=True, stop=True), r=[('g_kbe', c, h), k_('AiT')], w=['pkq'])
                S.op('act', lambda e: e.activation(nwT[p][:], pkq[:, 256:384], AF.Copy, scale=-1.0), r=['pkq'], w=[k_('nwT')])
                stage(10)
                pscn = pscan[p]
                S.op('pe', lambda e, c=c, h=h: e.matmul(pscn[:, 0:128], AiT[p][:], bv[:, c, h, :], start=True, stop=False), r=[k_('AiT'), ('g_bv', c, h)], w=[k_('pscan')], sig=False)
                S.op('pe', lambda e, h=h: e.matmul(pscn[:, 0:128], nwT[p][:], S16[:, h, :], start=False, stop=True), r=[k_('nwT'), ('g_S16', h)], w=[k_('pscan')])
                S.op('act', lambda e: e.copy(vn_b[p][:], pscn[:, 0:128]), r=[k_('pscan')], w=[k_('vnb')])
                S.op('pe', lambda e, h=h: e.matmul(pscn[:, 128:256], qdT[p][:], S16[:, h, :], start=True, stop=False), r=[k_('qdT'), ('g_S16', h)], w=[k_('pscan')], sig=False)
                S.op('pe', lambda e: e.matmul(pscn[:, 128:256], attnT[p][:], vn_b[p][:], start=False, stop=True), r=[k_('attnT'), k_('vnb')], w=[k_('pscan')])
                S.op('pe', lambda e, c=c, h=h: e.matmul(pscn[:, 256:384], kdec[:, c, h, :], vn_b[p][:], start=True, stop=True), r=[('g_kdec', c, h), k_('vnb')], w=[k_('pscan')])
                S.op('dve', lambda e, h=h, c=c: e.scalar_tensor_tensor(S32[:, h, :], S32[:, h, :], egc_bc[:, h, c * CH + CH - 1:c * CH + CH], pscn[:, 256:384], ALU.mult, ALU.add),
                     r=[k_('pscan'), ('g_S32', h), ('g_egcbc', h)], w=[('g_S32', h)])
                S.op('act', lambda e, h=h: e.copy(S16[:, h, :], S32[:, h, :]), r=[('g_S32', h)], w=[('g_S16', h)])
                stage(11)
                S.op('act', lambda e, h=h: e.activation(junk[p][:], pscn[:, 128:256], AF.Square, accum_out=ss[p][:, h:h + 1]), r=[k_('pscan')], w=[k_('junk'), (k_('ss'), h)])
                S.op('dve', lambda e, h=h: e.tensor_scalar(ss[p][:, h:h + 1], ss[p][:, h:h + 1], 1.0 / 128.0, RMS_EPS, ALU.mult, ALU.add), r=[(k_('ss'), h)], w=[(k_('ss'), h)])
                S.op('act', lambda e, h=h: e.activation(ss[p][:, h:h + 1], ss[p][:, h:h + 1], AF.Sqrt), r=[(k_('ss'), h)], w=[(k_('ss'), h)])
                S.op('dve', lambda e, h=h: e.reciprocal(ss[p][:, h:h + 1], ss[p][:, h:h + 1]), r=[(k_('ss'), h)], w=[(k_('ss'), h)])
                S.op('dve', lambda e, h=h, c=c, og_t=og_t: e.scalar_tensor_tensor(og_t[:, h * 128:(h + 1) * 128], pscn[:, 128:256], ss[p][:, h:h + 1], zg[:, c, h * 128:(h + 1) * 128], ALU.mult, ALU.mult),
                     r=[k_('pscan'), (k_('ss'), h), ('g_zg', c)], w=[('g_ogt', c % 2, h)])
            S.dma('sp', og[t0 + c * CH:t0 + (c + 1) * CH, :], og_t[:], r=[('g_ogt', c % 2, h) for h in range(4)], w=[('og', blk, c)])


ALPHA = 8.0 ** 0.25
LN_EPS = 1e-5


def oln_phase(nc, S, es, C, og_parts, xin, wout, lng, lnb, xout, t_start=0, t_end=8192, tag='o'):
    sb = lambda name, shape, dt: es.enter_context(nc.sbuf_tensor(tag + name, shape, dt))
    ps = lambda name, shape, dt: es.enter_context(nc.psum_tensor(tag + name, shape, dt))
    idb = C['idb']
    wstage = [sb("_wst%d" % i, [128, 768], F32) for i in range(2)]
    wout_b = sb("_wout", [128, 8, 1024], BF16)
    lng_s = sb("_lng", [128, 1024], F32)
    lnb_s = sb("_lnb", [128, 1024], F32)
    ot = [sb("_ot%d" % i, [128, 1024], BF16) for i in range(2)]
    oT = [sb("_oT%d" % i, [128, 8, 128], BF16) for i in range(2)]
    xt = [sb("_xt%d" % i, [128, 1024], F32) for i in range(2)]
    rr = [sb("_r%d" % i, [128, 1024], F32) for i in range(2)]
    xc = sb("_xc", [128, 1024], F32)
    junk = sb("_junk", [128, 1024], F32)
    st = [sb("_st%d" % i, [128, 4], F32) for i in range(2)]
    yo = [sb("_yo%d" % i, [128, 1024], F32) for i in range(2)]
    pT = ps("_pT", [128, 1024], BF16)
    py = [[ps("_py%d_%d" % (i, j), [128, 512], F32) for j in range(2)] for i in range(2)]
    wv = wout.rearrange("(kc p) n -> p kc n", p=128)
    i = 0
    for kc in range(8):
        for c0 in (0, 512):
            stg = wstage[i % 2]
            S.dma('sp', stg[:, 0:512], wv[:, kc, c0:c0 + 512], w=[(tag + 'wst', i % 2)])
            S.op('pool' if i % 2 == 0 else 'dve', lambda e, stg=stg, kc=kc, c0=c0: e.tensor_copy(wout_b[:, kc, c0:c0 + 512], stg[:, 0:512]), r=[(tag + 'wst', i % 2)], w=[(tag + 'wout', kc)])
            i += 1
    S.dma('sp', lng_s[:], lng, w=[tag + 'lng'])
    S.dma('sp', lnb_s[:], lnb, w=[tag + 'lnb'])
    WO = [(tag + 'wout', kc) for kc in range(8)]
    for ti, tok in enumerate(range(t_start, t_end, 128)):
        p = ti % 2
        K = lambda s: (tag + s, p)
        S.dma('sp', ot[p][:, 0:512], og_parts[0][tok:tok + 128, :], w=[K('ot')])
        S.dma('sp', ot[p][:, 512:1024], og_parts[1][tok:tok + 128, :], w=[K('ot')])
        S.dma('sp', xt[p][:], xin[tok:tok + 128, :], w=[K('xt')])
        for kc in range(8):
            S.op('pe', lambda e, kc=kc: e.transpose(pT[:, kc * 128:(kc + 1) * 128], ot[p][:, kc * 128:(kc + 1) * 128], idb[:]), r=[K('ot'), 'c_idb'], w=['p' + tag + 'T'], sig=(kc == 7))
        S.op('act', lambda e: e.copy(oT[p][:].rearrange("p a b -> p (a b)"), pT[:, :]), r=['p' + tag + 'T'], w=[K('oT')])
        for half in range(2):
            for kc in range(8):
                S.op('pe', lambda e, kc=kc, half=half: e.matmul(py[p][half][:, :], oT[p][:, kc, :], wout_b[:, kc, half * 512:(half + 1) * 512], start=(kc == 0), stop=(kc == 7)),
                     r=[K('oT'), (tag + 'wout', kc)], w=[('p' + tag + 'y', p, half)], sig=(kc == 7))
            S.op('dve', lambda e, half=half: e.scalar_tensor_tensor(rr[p][:, half * 512:(half + 1) * 512], xt[p][:, half * 512:(half + 1) * 512], ALPHA, py[p][half][:, :], ALU.mult, ALU.add),
                 r=[K('xt'), ('p' + tag + 'y', p, half)], w=[K('r')])
        s_ = st[p]
        S.op('dve', lambda e: e.reduce_sum(s_[:, 0:1], rr[p][:], AX.X), r=[K('r')], w=[K('st')])
        S.op('dve', lambda e: e.tensor_scalar(s_[:, 0:1], s_[:, 0:1], -1.0 / 1024.0, None, ALU.mult), r=[K('st')], w=[K('st')])
        S.op('pool', lambda e: e.tensor_scalar(xc[:], rr[p][:], s_[:, 0:1], None, ALU.add), r=[K('r'), K('st')], w=[tag + 'xc'])
        S.op('act', lambda e: e.activation(junk[:], xc[:], AF.Square, accum_out=s_[:, 1:2]), r=[tag + 'xc'], w=[tag + 'junk', K('st')])
        S.op('dve', lambda e: e.tensor_scalar(s_[:, 1:2], s_[:, 1:2], 1.0 / 1024.0, LN_EPS, ALU.mult, ALU.add), r=[K('st')], w=[K('st')])
        S.op('act', lambda e: e.activation(s_[:, 1:2], s_[:, 1:2], AF.Sqrt), r=[K('st')], w=[K('st')])
        S.op('dve', lambda e: e.reciprocal(s_[:, 1:2], s_[:, 1:2]), r=[K('st')], w=[K('st')])
        S.op('dve', lambda e: e.scalar_tensor_tensor(yo[p][:], xc[:], s_[:, 1:2], lng_s[:], ALU.mult, ALU.mult), r=[tag + 'xc', K('st'), tag + 'lng'], w=[K('yo')])
        S.op('pool', lambda e: e.tensor_tensor(yo[p][:], yo[p][:], lnb_s[:], ALU.add), r=[K('yo'), tag + 'lnb'], w=[K('yo')])
        S.dma('sp', xout[tok:tok + 128, :], yo[p][:], r=[K('yo')], w=[(tag + 'xout', ti)])


def load_xT(nc, S, C, xdram_blk, xt, xT, pbig, tag, nsub=4, F=1024, xT32=None, xms=None, xms_col=None):
    S.dma('sp', xt[:, 0:nsub, :], xdram_blk.rearrange("(s p) f -> p s f", p=128), w=[tag + '_xt'])
    nk = F // 128
    for kc in range(nk):
        pb = pbig[kc % 2]
        pk = ('pbig', kc % 2)
        for s in range(nsub):
            S.op('pe', lambda e, pb=pb, s=s, kc=kc: e.transpose(pb[:, s * 128:(s + 1) * 128], xt[:, s, kc * 128:(kc + 1) * 128], C['idf'][:]),
                 r=[tag + '_xt', 'c_idf'], w=[pk], sig=(s == nsub - 1))
        if kc % 2 == 0:
            S.op('act', lambda e, pb=pb, kc=kc: e.copy(xT[:, kc, 0:nsub * 128], pb[:, 0:nsub * 128]), r=[pk], w=[(tag + '_xT', kc)])
        else:
            S.op('dve', lambda e, pb=pb, kc=kc: e.tensor_copy(xT[:, kc, 0:nsub * 128], pb[:, 0:nsub * 128]), r=[pk], w=[(tag + '_xT', kc)])
        if xT32 is not None:
            S.op('dve', lambda e, pb=pb, kc=kc: e.tensor_copy(xT32[:, kc, 0:nsub * 128], pb[:, 0:nsub * 128]), r=[pk], w=[(tag + '_xT32', kc)])
        if xms is not None:
            S.op('dve', lambda e, pb=pb, kc=kc: e.reduce_sum(xms[:, kc, xms_col:xms_col + nsub // 2], pb[:, 0:nsub * 128].rearrange("p (a b) -> p a b", b=256), AX.X), r=[pk], w=[(tag + '_xms', kc)])

T = 8192
BLK = 256
NB = T // BLK
BIG = 30000.0
QSCALE = 128.0 ** -0.5


def _load_w(nc, S, stage, wdram, wsb, name, skey):
    wv = wdram.rearrange("(kc p) n -> p kc n", p=128)
    for kc in range(8):
        st = stage[kc % 2]
        S.dma('sp', st[:, :], wv[:, kc, :], w=[(skey, kc % 2)])
        S.op('pool' if kc % 2 == 0 else 'dve', lambda e, st=st, kc=kc: e.tensor_copy(wsb[:, kc, :], st[:, :]), r=[(skey, kc % 2)], w=[(name, kc)])


def moba_phase(nc, S, es, C, xkv, xq, wk, wv, wq, wz, og, nqb=NB, nkvb=NB):
    sb = lambda name, shape, dt: es.enter_context(nc.sbuf_tensor("m_" + name, shape, dt))
    ps = lambda name, shape, dt: es.enter_context(nc.psum_tensor("m_" + name, shape, dt))
    idf, idb = C['idf'], C['idb']
    KT = sb("KT", [128, 4, T], BF16)
    VA = sb("VA", [128, T // 128, 4, 129], BF16)
    xms = sb("xms", [128, 8, NB], F32)
    kmT32 = sb("kmT32", [128, 4, NB], F32)
    G32 = sb("G32", [128, 8, 128], F32)
    xT32 = sb("xT32", [128, 8, BLK], F32)
    kmax2 = sb("kmax2", [128, 4], F32)
    xt = sb("xt", [128, 2, 1024], F32)
    xT = sb("xT", [128, 8, BLK], BF16)
    wst = [sb("wst%d" % i, [128, 512], F32) for i in range(2)]
    mincl_b = sb("minclb", [128, 128], BF16)
    pbig = [ps("pbig%d" % i, [128, 512], F32) for i in range(2)]
    S.op('pool', lambda e: e.tensor_copy(mincl_b[:], C['mincl'][:]), r=['c_mincl'], w=['m_minclb'])
    S.op('pool', lambda e: e.memset(VA[:, :, :, 128:129], 1.0), w=['m_VAones'])
    S.op('pool', lambda e: e.memset(kmax2[:], 0.0), w=['m_kmax2'])
    S.op('pool', lambda e: e.memset(xms[:], 0.0), w=[('m_xms', kc) for kc in range(8)])
    with ExitStack() as es2:
        sb2 = lambda name, shape, dt: es2.enter_context(nc.sbuf_tensor("m_" + name, shape, dt))
        wk_b = sb2("wk", [128, 8, 512], BF16)
        wv_b = sb2("wv", [128, 8, 512], BF16)
        ksq = sb2("ksq", [128, BLK], BF16)
        mx = sb2("mx", [128, 2], F32)
        _load_w(nc, S, wst, wk, wk_b, 'm_wk', 'm_wst')
        _load_w(nc, S, wst, wv, wv_b, 'm_wv', 'm_wst')
        for blk in range(nkvb):
            t0 = blk * BLK
            load_xT(nc, S, C, xkv[t0:t0 + BLK, :], xt, xT, pbig, 'm', nsub=2, xms=xms, xms_col=blk)
            for h in range(4):
                pb = pbig[h % 2]
                pk = ('pbig', h % 2)
                for kc in range(8):
                    S.op('pe', lambda e, pb=pb, kc=kc, h=h: e.matmul(pb[:, 0:BLK], wk_b[:, kc, h * 128:(h + 1) * 128], xT[:, kc, :], start=(kc == 0), stop=(kc == 7)),
                         r=[('m_wk', kc), ('m_xT', kc)], w=[pk], sig=(kc == 7))
                S.op('act', lambda e, pb=pb, h=h: e.copy(KT[:, h, t0:t0 + BLK], pb[:, 0:BLK]), r=[pk], w=[('m_KT', h, blk)])
                S.op('pool', lambda e, h=h: e.tensor_tensor(ksq[:], KT[:, h, t0:t0 + BLK], KT[:, h, t0:t0 + BLK], ALU.mult), r=[('m_KT', h, blk)], w=['m_ksq'])
                S.op('pe', lambda e, pb=pb: e.matmul(pb[:, 0:BLK], C['onesb'][:], ksq[:], start=True, stop=True), r=['m_ksq', 'c_onesb'], w=[pk])
                S.op('dve', lambda e, pb=pb: e.reduce_max(mx[:, 0:1], pb[:, 0:BLK], AX.X), r=[pk], w=['m_mx'])
                S.op('dve', lambda e, h=h: e.tensor_tensor(kmax2[:, h:h + 1], kmax2[:, h:h + 1], mx[:, 0:1], ALU.max), r=['m_mx', 'm_kmax2'], w=['m_kmax2'])
            for s in range(2):
                pb = pbig[s % 2]
                pk = ('pbig', s % 2)
                for kc in range(8):
                    S.op('pe', lambda e, pb=pb, kc=kc, s=s: e.matmul(pb[:, :], xT[:, kc, s * 128:(s + 1) * 128], wv_b[:, kc, :], start=(kc == 0), stop=(kc == 7)),
                         r=[('m_wv', kc), ('m_xT', kc)], w=[pk], sig=(kc == 7))
                ti = blk * 2 + s
                S.op('act' if s % 2 == 0 else 'dve',
                     (lambda e, pb=pb, ti=ti: e.copy(VA[:, ti, :, 0:128], pb[:, :].rearrange("p (a b) -> p a b", b=128))) if s % 2 == 0 else
                     (lambda e, pb=pb, ti=ti: e.tensor_copy(VA[:, ti, :, 0:128], pb[:, :].rearrange("p (a b) -> p a b", b=128))),
                     r=[pk], w=[('m_VA', ti)])
    S.barrier()
    with ExitStack() as es3:
        sb3 = lambda name, shape, dt: es3.enter_context(nc.sbuf_tensor("m_" + name, shape, dt))
        w32 = sb3("w32", [128, 8, 512], F32)
        wqT = sb3("wqT", [128, 1024], F32)
        S.dma('sp', w32[:], wk.rearrange("(kc p) n -> p kc n", p=128), w=['m_w32'])
        for h in range(4):
            for kc in range(8):
                S.op('pe', lambda e, h=h, kc=kc: e.matmul(pbig[0][:, h * 32:(h + 1) * 32], w32[:, kc, h * 128:(h + 1) * 128], xms[:, kc, :], start=(kc == 0), stop=(kc == 7)),
                     r=['m_w32', ('m_xms', kc)], w=[('pbig', 0)], sig=(kc == 7))
        S.op('dve', lambda e: e.tensor_scalar(kmT32[:].rearrange("p a b -> p (a b)"), pbig[0][:, 0:128], 1.0 / 256.0, None, ALU.mult), r=[('pbig', 0)], w=['m_kmT32'])
        S.dma('sp', w32[:], wq.rearrange("(kc p) n -> p kc n", p=128), w=['m_w32'])
        for h in range(4):
            for kc in range(8):
                pb = pbig[kc % 2]
                S.op('pe', lambda e, pb=pb, h=h, kc=kc: e.transpose(pb[:, 0:128], w32[:, kc, h * 128:(h + 1) * 128], idf[:]), r=['m_w32', 'c_idf'], w=[('pbig', kc % 2)])
                S.op('dve', lambda e, pb=pb, kc=kc: e.tensor_copy(wqT[:, kc * 128:(kc + 1) * 128], pb[:, 0:128]), r=[('pbig', kc % 2)], w=['m_wqT'])
            for kc in range(8):
                pb = pbig[kc % 2]
                S.op('pe', lambda e, pb=pb, h=h, kc=kc: e.matmul(pb[:, 0:32], wqT[:, kc * 128:(kc + 1) * 128], kmT32[:, h, :], start=True, stop=True),
                     r=['m_wqT', 'm_kmT32'], w=[('pbig', kc % 2)])
                S.op('dve', lambda e, pb=pb, kc=kc, h=h: e.tensor_copy(G32[:, kc, h * 32:(h + 1) * 32], pb[:, 0:32]), r=[('pbig', kc % 2)], w=['m_G32'])
    S.barrier()
    wq_b = sb("wq", [128, 8, 512], BF16)
    wz_b = sb("wz", [128, 8, 512], BF16)
    EN = sb("EN", [32, 32, 128], BF16)
    qT_b = sb("qTb", [128, 4, BLK], BF16)
    qsq = sb("qsq", [128, BLK], BF16)
    zs = [sb("zs%d" % i, [128, 512], F32) for i in range(2)]
    gsb = sb("gsb", [128, 40], F32)
    top8 = sb("top8", [128, 8], F32)
    mcol = sb("mcol", [128, 4], F32)
    bTM = sb("bTM", [128, 32], F32)
    biasT = [sb("biasT%d" % i, [32, BLK], BF16) for i in range(2)]
    pT_sb = [sb("pTsb%d" % i, [128, BLK], BF16) for i in range(3)]
    rinv = sb("rinv", [128, 4], F32)
    ogt = [sb("ogt%d" % i, [128, 512], BF16) for i in range(2)]
    pst = [ps("pst%d" % i, [128, 512], F32) for i in range(2)]
    po = [ps("po%d" % i, [128, 512], F32) for i in range(2)]
    pg = ps("pg", [128, 512], F32)
    pgate = ps("pgate", [128, 512], F32)
    gate_sb = sb("gate", [128, 2, 128], F32)
    _load_w(nc, S, wst, wq, wq_b, 'm_wq', 'm_wst')
    _load_w(nc, S, wst, wz, wz_b, 'm_wz', 'm_wst')
    S.op('pool', lambda e: e.memset(EN[:], 1.0), w=['m_EN'])
    S.op('pool', lambda e: e.affine_select(EN[:], EN[:], [[-1, 32], [0, 128]], ALU.is_equal, 0.0, base=0, channel_multiplier=1), r=['m_EN'], w=['m_EN'])
    ist = 0
    for i in range(nqb):
        t0 = i * BLK
        load_xT(nc, S, C, xq[t0:t0 + BLK, :], xt, xT, pbig, 'm', nsub=2, xT32=xT32)
        for h in range(4):
            pb = pbig[h % 2]
            pk = ('pbig', h % 2)
            for kc in range(8):
                S.op('pe', lambda e, pb=pb, kc=kc, h=h: e.matmul(pb[:, 0:BLK], wq_b[:, kc, h * 128:(h + 1) * 128], xT[:, kc, 0:BLK], start=(kc == 0), stop=(kc == 7)),
                     r=[('m_wq', kc), ('m_xT', kc)], w=[pk], sig=(kc == 7))
            S.op('act', lambda e, pb=pb, h=h: e.activation(qT_b[:, h, :], pb[:, 0:BLK], AF.Copy, scale=QSCALE), r=[pk], w=[('m_qT', h)])
        for s in range(2):
            pb = pbig[s % 2]
            pk = ('pbig', s % 2)
            for kc in range(8):
                S.op('pe', lambda e, pb=pb, kc=kc, s=s: e.matmul(pb[:, :], xT[:, kc, s * 128:(s + 1) * 128], wz_b[:, kc, :], start=(kc == 0), stop=(kc == 7)),
                     r=[('m_wz', kc), ('m_xT', kc)], w=[pk], sig=(kc == 7))
            S.op('act', lambda e, pb=pb, s=s: e.activation(zs[s][:], pb[:, :], AF.Silu), r=[pk], w=[('m_zs', s)])
        for qt in range(2):
            for kc in range(8):
                S.op('pe', lambda e, qt=qt, kc=kc: e.matmul(pgate[:, qt * 128:(qt + 1) * 128], xT32[:, kc, qt * 128:(qt + 1) * 128], G32[:, kc, :], start=(kc == 0), stop=(kc == 7)),
                     r=[('m_xT32', kc), 'm_G32'], w=['pgate'], sig=(kc == 7))
        S.op('dve', lambda e: e.tensor_copy(gate_sb[:].rearrange("p a b -> p (a b)"), pgate[:, 0:256]), r=['pgate'], w=['m_gate'])
        for h in range(4):
            hp_ = h % 2
            bT = biasT[hp_]
            S.op('pool', lambda e, h=h: e.tensor_tensor(qsq[:], qT_b[:, h, :], qT_b[:, h, :], ALU.mult), r=[('m_qT', h)], w=['m_qsq'])
            for qt in range(2):
                qs = slice(qt * 128, (qt + 1) * 128)
                S.op('pe', lambda e, qs=qs: e.matmul(pg[:, 32:33], qsq[:, qs], C['onesb'][:, 0:1], start=True, stop=True), r=['m_qsq', 'c_onesb'], w=['pg'])
                S.op('pool', lambda e: e.memset(gsb[:, 0:33], -1e30), w=['m_gsb'])
                if i > 0:
                    S.op('dve', lambda e, h=h, qt=qt: e.tensor_copy(gsb[:, 0:i], gate_sb[:, qt, h * 32:h * 32 + i]), r=['m_gate'], w=['m_gsb'])
                S.op('dve', lambda e, h=h: e.tensor_scalar(mcol[:, 0:1], pg[:, 32:33], kmax2[:, h:h + 1], None, ALU.mult), r=['pg', 'm_kmax2'], w=['m_mcol'])
                S.op('act', lambda e: e.activation(mcol[:, 0:1], mcol[:, 0:1], AF.Sqrt), r=['m_mcol'], w=['m_mcol'])
                S.op('dve', lambda e: e.tensor_scalar(mcol[:, 1:2], mcol[:, 0:1], -1.0, None, ALU.mult), r=['m_mcol'], w=['m_mcol'])
                S.op('dve', lambda e: e.tensor_scalar(mcol[:, 2:3], mcol[:, 0:1], -1.0, -BIG, ALU.mult, ALU.add), r=['m_mcol'], w=['m_mcol'])
                if i >= 4:
                    S.op('dve', lambda e: e.max(top8[:], gsb[:, 0:max(i, 8)]), r=['m_gsb'], w=['m_top8'])
                    S.op('dve', lambda e: e.tensor_scalar(bTM[:, 0:i], gsb[:, 0:i], top8[:, 2:3], BIG, ALU.is_ge, ALU.mult), r=['m_gsb', 'm_top8'], w=['m_bTM'])
                    S.op('dve', lambda e: e.tensor_scalar(bTM[:, 0:i], bTM[:, 0:i], mcol[:, 2:3], None, ALU.add), r=['m_bTM', 'm_mcol'], w=['m_bTM'])
                elif i > 0:
                    S.op('dve', lambda e: e.tensor_scalar(bTM[:, 0:i], gsb[:, 0:i], 0.0, mcol[:, 1:2], ALU.mult, ALU.add), r=['m_gsb', 'm_mcol'], w=['m_bTM'])
                S.op('dve', lambda e: e.tensor_copy(bTM[:, i:i + 1], mcol[:, 1:2]), r=['m_mcol'], w=['m_bTM'])
                S.op('pe', lambda e: e.transpose(pg[0:i + 1, 64:192], bTM[:, 0:i + 1], idf[:]), r=['m_bTM', 'c_idf'], w=['pg'])
                S.op('act', lambda e, qs=qs, bT=bT: e.copy(bT[0:i + 1, qs], pg[0:i + 1, 64:192]), r=['pg'], w=[('m_biasT', hp_)])
            for n in range(i + 1):
                for kt in range(2):
                    Tk = n * 2 + kt
                    a = ist % 2
                    ist += 1
                    psk = ('pst', a)
                    tsb = pT_sb[ist % 3]
                    tk = ('m_pTsb', ist % 3)
                    S.op('pe', lambda e, a=a, Tk=Tk, h=h: e.matmul(pst[a][:, 0:BLK], KT[:, h, Tk * 128:(Tk + 1) * 128], qT_b[:, h, :], start=True, stop=False),
                         r=[('m_KT', h, Tk // 2), ('m_qT', h)], w=[psk], sig=False)
                    S.op('pe', lambda e, a=a, n=n, bT=bT: e.matmul(pst[a][:, 0:BLK], EN[0:i + 1, n, :], bT[0:i + 1, :], start=False, stop=True),
                         r=['m_EN', ('m_biasT', hp_)], w=[psk])
                    S.op('act', lambda e, a=a, tsb=tsb: e.activation(tsb[:], pst[a][:, 0:BLK], AF.Exp), r=[psk], w=[tk])
                    if n == i:
                        qd = slice(kt * 128, (kt + 1) * 128)
                        S.op('pool', lambda e, tsb=tsb, qd=qd: e.tensor_tensor(tsb[:, qd], tsb[:, qd], mincl_b[:], ALU.mult), r=[tk, 'm_minclb'], w=[tk])
                    for qt in range(2):
                        if n == i and kt > qt:
                            continue
                        first = (Tk == 0)
                        last = (n == i and kt == qt)
                        S.op('pe', lambda e, qt=qt, tsb=tsb, Tk=Tk, h=h, first=first, last=last: e.matmul(po[qt][:, 0:129], tsb[:, qt * 128:(qt + 1) * 128], VA[:, Tk, h, :], start=first, stop=last),
                             r=[tk, ('m_VA', Tk), 'm_VAones'], w=[('po', qt)], sig=last)
            for qt in range(2):
                S.op('dve', lambda e, qt=qt, h=h: e.reciprocal(rinv[:, h:h + 1], po[qt][:, 128:129]), r=[('po', qt)], w=[('m_rinv', h)])
                S.op('dve', lambda e, qt=qt, h=h: e.scalar_tensor_tensor(ogt[qt][:, h * 128:(h + 1) * 128], po[qt][:, 0:128], rinv[:, h:h + 1], zs[qt][:, h * 128:(h + 1) * 128], ALU.mult, ALU.mult),
                     r=[('po', qt), ('m_rinv', h), ('m_zs', qt)], w=[('m_ogt', qt, h)])
        for qt in range(2):
            S.dma('sp', og[t0 + qt * 128:t0 + (qt + 1) * 128, :], ogt[qt][:], r=[('m_ogt', qt, h) for h in range(4)], w=[('og', i, qt)])


from concourse.bass_utils import run_bass_kernel_spmd

N_A = 2
N_CORES = 8


def _dram(nc, name, shape, dt, kind):
    return nc.dram_tensor(name, shape, dt, kind=kind).ap()


def _decl_gdn(nc):
    d = {}
    d['wqkv'] = _dram(nc, "wqkv", [1024, 1536], F32, "ExternalInput")
    d['wz'] = _dram(nc, "wz", [1024, 512], F32, "ExternalInput")
    d['wab'] = _dram(nc, "wab", [1024, 8], F32, "ExternalInput")
    d['convw'] = _dram(nc, "convw", [128, 12, 4], F32, "ExternalInput")
    d['hp'] = _dram(nc, "hp", [4, 2], F32, "ExternalInput")
    d['normw'] = _dram(nc, "normw", [128, 512], F32, "ExternalInput")
    return d


def _decl_oln(nc, full=True):
    d = {}
    if full:
        d['og0'] = _dram(nc, "og0", [T, 512], BF16, "ExternalInput")
        d['og1'] = _dram(nc, "og1", [T, 512], BF16, "ExternalInput")
    d['wout'] = _dram(nc, "wout", [1024, 1024], F32, "ExternalInput")
    d['lng'] = _dram(nc, "lng", [128, 1024], F32, "ExternalInput")
    d['lnb'] = _dram(nc, "lnb", [128, 1024], F32, "ExternalInput")
    return d


def _decl_moba(nc):
    d = {}
    for n in ('wk', 'wv', 'wq', 'wz'):
        d[n] = _dram(nc, "m" + n, [1024, 512], F32, "ExternalInput")
    return d


def build_launch(kind):
    nc = bass.Bass("TRN2", target_bir_lowering=False)
    if kind != 'oln_final':
        xin = _dram(nc, "xin", [T, 1024], F32, "ExternalInput")
    with ExitStack() as es:
        S = Sched(nc, es)
        C = setup_consts(nc, S, es)
        if kind == 'gdn':
            g = _decl_gdn(nc)
            og = _dram(nc, "og", [T, 512], BF16, "ExternalOutput")
            with ExitStack() as es1:
                gdn_phase(nc, S, es1, C, xin, g['wqkv'], g['wz'], g['wab'], g['convw'], g['hp'], g['normw'], og)
        elif kind == 'oln_final':
            o = _decl_oln(nc, full=False)
            xout = _dram(nc, "xout", [T // 2, 1024], F32, "ExternalOutput")
            xin_h = _dram(nc, "xin_h", [T // 2, 1024], F32, "ExternalInput")
            og0h = _dram(nc, "og0h", [T // 2, 512], BF16, "ExternalInput")
            og1h = _dram(nc, "og1h", [T // 2, 512], BF16, "ExternalInput")
            with ExitStack() as es1:
                oln_phase(nc, S, es1, C, [og0h, og1h], xin_h, o['wout'], o['lng'], o['lnb'], xout, 0, T // 2)
        else:
            o = _decl_oln(nc)
            xout = _dram(nc, "xout", [T, 1024], F32, "ExternalOutput")
            og = _dram(nc, "og", [T, 512], BF16, "ExternalOutput")
            with ExitStack() as es1:
                oln_phase(nc, S, es1, C, [o['og0'], o['og1']], xin, o['wout'], o['lng'], o['lnb'], xout, 0, T)
            S.barrier()
            if kind == 'oln_gdn':
                g = _decl_gdn(nc)
                with ExitStack() as es1:
                    gdn_phase(nc, S, es1, C, xout, g['wqkv'], g['wz'], g['wab'], g['convw'], g['hp'], g['normw'], og)
            else:
                m = _decl_moba(nc)
                if kind == 'oln_moba':
                    xkv = xout
                else:
                    xkv = _dram(nc, "xkv", [T, 1024], F32, "ExternalInput")
                with ExitStack() as es1:
                    moba_phase(nc, S, es1, C, xkv, xout, m['wk'], m['wv'], m['wq'], m['wz'], og)
        S.finish()
    return nc


def _c(a):
    return np.ascontiguousarray(a)


def gdn_inputs(inp, layer, hg):
    w_in = inp['a_w_in'][layer]
    hs = slice(hg * 512, (hg + 1) * 512)
    wq, wk, wv, wz_ = (w_in[:, o * 1024:(o + 1) * 1024][:, hs] for o in range(4))
    wa = w_in[:, 4096 + 8 + hg * 4: 4096 + 8 + hg * 4 + 4]
    wb = w_in[:, 4096 + hg * 4: 4096 + hg * 4 + 4]
    cw = inp['a_conv_w'][layer]
    cws = np.concatenate([cw[:, o * 1024:(o + 1) * 1024][:, hs] for o in range(3)], 1)
    return {"wqkv": _c(np.concatenate([wq, wk, wv], 1)), "wz": _c(wz_), "wab": _c(np.concatenate([wa, wb], 1)),
            "convw": _c(cws.reshape(4, 12, 128).transpose(2, 1, 0)),
            "hp": _c(np.stack([inp['a_dt_bias'][layer][hg * 4:hg * 4 + 4], inp['a_A_log'][layer][hg * 4:hg * 4 + 4]], 1)),
            "normw": _c(np.tile(inp['a_norm_w'][layer][None, :], (128, 4)))}


def oln_inputs(wout, g, b):
    return {"wout": _c(wout), "lng": _c(np.tile(g[None], (128, 1))), "lnb": _c(np.tile(b[None], (128, 1)))}


def moba_inputs(inp, li, hg):
    hs = slice(hg * 512, (hg + 1) * 512)
    wkv = inp['b_w_kv']
    win = inp['b_w_in'][li]
    return {"mwk": _c(wkv[:, :1024][:, hs]), "mwv": _c(wkv[:, 1024:][:, hs]), "mwq": _c(win[:, :1024][:, hs]), "mwz": _c(win[:, 1024:][:, hs])}


def _run(nc, in_maps):
    res = run_bass_kernel_spmd(nc, in_maps, core_ids=list(range(N_CORES)))
    return res.results


def kernel(**inputs):
    inp = {k: np.asarray(v) for k, v in inputs.items()}
    x = inp['x'].astype(np.float32, copy=False)
    cores = [(c // 2, c % 2) for c in range(N_CORES)]
    nc = build_launch('gdn')
    r = _run(nc, [dict(gdn_inputs(inp, 0, hg), xin=_c(x[b])) for b, hg in cores])
    og = [r[c]["og"] for c in range(N_CORES)]
    xs = [x[b] for b in range(4)]
    nc = build_launch('oln_gdn')
    r = _run(nc, [dict(gdn_inputs(inp, 1, hg), **oln_inputs(inp['a_w_out'][0], inp['a_ln_g'][0], inp['a_ln_b'][0]),
                       xin=_c(xs[b]), og0=og[2 * b], og1=og[2 * b + 1]) for b, hg in cores])
    og = [r[c]["og"] for c in range(N_CORES)]
    xs = [r[2 * b]["xout"] for b in range(4)]
    nc = build_launch('oln_moba')
    r = _run(nc, [dict(moba_inputs(inp, 0, hg), **oln_inputs(inp['a_w_out'][1], inp['a_ln_g'][1], inp['a_ln_b'][1]),
                       xin=_c(xs[b]), og0=og[2 * b], og1=og[2 * b + 1]) for b, hg in cores])
    og = [r[c]["og"] for c in range(N_CORES)]
    x2 = [r[2 * b]["xout"] for b in range(4)]
    nc = build_launch('oln_moba2')
    r = _run(nc, [dict(moba_inputs(inp, 1, hg), **oln_inputs(inp['b_w_out'][0], inp['b_ln_g'][0], inp['b_ln_b'][0]),
                       xin=_c(x2[b]), xkv=_c(x2[b]), og0=og[2 * b], og1=og[2 * b + 1]) for b, hg in cores])
    og = [r[c]["og"] for c in range(N_CORES)]
    x3 = [r[2 * b]["xout"] for b in range(4)]
    nc = build_launch('oln_final')
    H = T // 2
    r = _run(nc, [dict(oln_inputs(inp['b_w_out'][1], inp['b_ln_g'][1], inp['b_ln_b'][1]),
                       xin_h=_c(x3[b][hg * H:(hg + 1) * H]), og0h=_c(og[2 * b][hg * H:(hg + 1) * H]), og1h=_c(og[2 * b + 1][hg * H:(hg + 1) * H]))
                  for b, hg in cores])
    out = np.empty((4, T, 1024), np.float32)
    for c, (b, hg) in enumerate(cores):
        out[b, hg * H:(hg + 1) * H] = r[c]["xout"]
    return out
```

```python
import numpy as np
from contextlib import ExitStack
import concourse.bass as bass
import concourse.mybir as mybir

F32 = mybir.dt.float32
BF16 = mybir.dt.bfloat16
ALU = mybir.AluOpType
AF = mybir.ActivationFunctionType
AX = mybir.AxisListType


class Sched:
    def __init__(self, nc, es, ndma=8):
        self.nc = nc
        self.es = es
        self.E = {'pe': nc.tensor, 'dve': nc.vector, 'act': nc.scalar, 'pool': nc.gpsimd, 'sp': nc.sync}
        self.csem = {e: es.enter_context(nc.semaphore("c_" + e)) for e in ('pe', 'dve', 'act', 'pool')}
        self.cnt = {e: 0 for e in self.csem}
        self.ndma = ndma
        self.dsem = {q: [es.enter_context(nc.semaphore("d_%s%d" % (q, i))) for i in range(ndma)]
                     for q in ('sp', 'pool', 'act')}
        self.dcnt = {q: 0 for q in self.dsem}
        self.seen = {e: {} for e in self.E}
        self.lastw = {}
        self.readers = {}
        self.nwaits = 0
        self.epoch = 0
        self.uid = 0

    def _sem(self, key):
        return self.csem[key[1]] if key[0] == 'c' else self.dsem[key[1]][key[2]]

    def _wait(self, e, tick):
        key, val = tick
        if self.seen[e].get(key, 0) >= val:
            return
        self.E[e].wait_ge(self._sem(key), val)
        self.seen[e][key] = val
        self.nwaits += 1

    def _deps(self, e, r, w):
        for k in r:
            t = self.lastw.get(k)
            if t is not None:
                self._wait(e, t)
        for k in w:
            t = self.lastw.get(k)
            if t is not None and (t[0] != ('c', e) or e != 'pe'):
                self._wait(e, t)
            for t in self.readers.get(k, {}).values():
                if t[0] != ('c', e) or e != 'pe':
                    self._wait(e, t)

    def _mark(self, tick, r, w):
        for k in r:
            self.readers.setdefault(k, {})[tick[0]] = tick
        for k in w:
            self.lastw[k] = tick
            self.readers[k] = {}

    @staticmethod
    def is_psum(k):
        while isinstance(k, tuple):
            k = k[0]
        return k.startswith('p')

    def op(self, e, fn, r=(), w=(), sig=True):
        w = list(w) + [k for k in r if self.is_psum(k)]
        r = [k for k in r if not self.is_psum(k)]
        self._deps(e, r, w)
        ins = fn(self.E[e])
        if sig:
            self.cnt[e] += 1
            ins.then_inc(self.csem[e], 1)
            tick = (('c', e), self.cnt[e])
        else:
            tick = (('c', e), self.cnt[e] + 1)
        self._mark(tick, r, w)
        return ins

    def dma(self, q, out, in_, r=(), w=(), **kw):
        self._deps(q, r, w)
        n = self.dcnt[q]
        slot = n % self.ndma
        key = ('d', q, slot)
        rnd = n // self.ndma
        if rnd > 0:
            self._wait(q, (key, 16 * rnd))
        self.E[q].dma_start(out=out, in_=in_, **kw).then_inc(self.dsem[q][slot], 16)
        self.dcnt[q] = n + 1
        self._mark((key, 16 * (rnd + 1)), r, w)

    def barrier(self):
        for e in self.E:
            for e2 in self.csem:
                if e2 != e and self.cnt[e2] > 0:
                    self._wait(e, (('c', e2), self.cnt[e2]))
            for q in self.dsem:
                n = self.dcnt[q]
                for slot in range(min(n, self.ndma)):
                    cnt = (n - slot + self.ndma - 1) // self.ndma
                    self._wait(e, (('d', q, slot), 16 * cnt))

    def dma_fn(self, q, fn, r=(), w=()):
        self._deps(q, r, w)
        n = self.dcnt[q]
        slot = n % self.ndma
        key = ('d', q, slot)
        rnd = n // self.ndma
        if rnd > 0:
            self._wait(q, (key, 16 * rnd))
        fn(self.E[q]).then_inc(self.dsem[q][slot], 16)
        self.dcnt[q] = n + 1
        self._mark((key, 16 * (rnd + 1)), r, w)

    def new_epoch(self):
        self.barrier()
        for e in self.csem:
            if self.cnt[e] > 0:
                self._wait(e, (('c', e), self.cnt[e]))
        self.epoch += 1
        self.csem = {e: self.es.enter_context(self.nc.semaphore("c%d_%s" % (self.epoch, e))) for e in ('pe', 'dve', 'act', 'pool')}
        self.cnt = {e: 0 for e in self.csem}
        for e in self.seen:
            for k in [k for k in self.seen[e] if k[0] == 'c']:
                del self.seen[e][k]
        self.lastw = {}
        self.readers = {}

    def finish(self):
        for q in self.dsem:
            n = self.dcnt[q]
            for slot in range(min(n, self.ndma)):
                last = ((n - 1 - slot) // self.ndma) + 1 if n - 1 >= slot else 0
                cnt = (n - slot + self.ndma - 1) // self.ndma
                self._wait(q, (('d', q, slot), 16 * cnt))


T = 8192
D = 1024
TB = 512
CH = 128
NLEV = 6
RMS_EPS = 1e-6


def setup_consts(nc, S, es):
    sb = lambda name, shape, dt: es.enter_context(nc.sbuf_tensor(name, shape, dt))
    C = {}
    ones = sb("c_ones", [128, 512], F32)
    idf = sb("c_idf", [128, 128], F32)
    idb = sb("c_idb", [128, 128], BF16)
    mincl = sb("c_mincl", [128, 128], F32)
    mstrict = sb("c_mstrict", [128, 128], F32)
    rmask = sb("c_rmask", [4, 4, 128], F32)
    esel = sb("c_esel", [4, 4, 128], F32)
    onesb = sb("c_onesb", [128, 128], BF16)
    S.op('pool', lambda e: e.memset(ones[:], 1.0), w=['c_ones'])
    S.op('pool', lambda e: e.memset(onesb[:], 1.0), w=['c_onesb'])
    S.op('pool', lambda e: e.affine_select(idf[:], ones[:, 0:128], [[-1, 128]], ALU.is_equal, 0.0, base=0, channel_multiplier=1), r=['c_ones'], w=['c_idf'])
    S.op('pool', lambda e: e.tensor_copy(idb[:], idf[:]), r=['c_idf'], w=['c_idb'])
    S.op('pool', lambda e: e.affine_select(mincl[:], ones[:, 0:128], [[1, 128]], ALU.is_ge, 0.0, base=0, channel_multiplier=-1), r=['c_ones'], w=['c_mincl'])
    S.op('pool', lambda e: e.affine_select(mstrict[:], ones[:, 0:128], [[1, 128]], ALU.is_gt, 0.0, base=0, channel_multiplier=-1), r=['c_ones'], w=['c_mstrict'])
    S.op('pool', lambda e: e.affine_select(rmask[:], ones[0:4, 0:512].rearrange("p (a b) -> p a b", b=128), [[0, 4], [1, 128]], ALU.not_equal, 0.0, base=0, channel_multiplier=0), r=['c_ones'], w=['c_rmask'])
    S.op('pool', lambda e: e.affine_select(esel[:], ones[0:4, 0:512].rearrange("p (a b) -> p a b", b=128), [[-1, 4], [0, 128]], ALU.is_equal, 0.0, base=0, channel_multiplier=1), r=['c_ones'], w=['c_esel'])
    C.update(ones=ones, idf=idf, idb=idb, mincl=mincl, mstrict=mstrict, rmask=rmask, esel=esel, onesb=onesb)
    return C


def load_cast_weight(nc, S, stage, wdram, wsb, ncols, name, engs=('pool', 'dve', 'act')):
    wv = wdram.rearrange("(kc p) n -> p kc n", p=128)
    CW = 768
    i = 0
    for kc in range(8):
        for c0 in range(0, ncols, CW):
            c1 = min(ncols, c0 + CW)
            st = stage[i % 2]
            S.dma('sp', st[:, 0:c1 - c0], wv[:, kc, c0:c1], w=[('g_qkvT', 2 * (i % 2)), ('g_qkvT', 2 * (i % 2) + 1)])
            eng = engs[i % len(engs)]
            if eng == 'act':
                S.op('act', lambda e, st=st, kc=kc, c0=c0, c1=c1: e.copy(wsb[:, kc, c0:c1], st[:, 0:c1 - c0]), r=[('g_qkvT', 2 * (i % 2)), ('g_qkvT', 2 * (i % 2) + 1)], w=[(name, kc)])
            else:
                S.op(eng, lambda e, st=st, kc=kc, c0=c0, c1=c1: e.tensor_copy(wsb[:, kc, c0:c1], st[:, 0:c1 - c0]), r=[('g_qkvT', 2 * (i % 2)), ('g_qkvT', 2 * (i % 2) + 1)], w=[(name, kc)])
            i += 1


def load_xT(nc, S, C, xdram_blk, xt, xT, pbig, tag, nsub=4, F=1024):
    S.dma('sp', xt[:, 0:nsub, :], xdram_blk.rearrange("(s p) f -> p s f", p=128), w=[tag + '_xt'])
    nk = F // 128
    for kc in range(nk):
        pb = pbig[kc % 2]
        for s in range(nsub):
            S.op('pe', lambda e, pb=pb, s=s, kc=kc: e.transpose(pb[:, s * 128:(s + 1) * 128], xt[:, s, kc * 128:(kc + 1) * 128], C['idf'][:]),
                 r=[tag + '_xt', 'c_idf'], w=[('pbig', kc % 2)], sig=(s == nsub - 1))
        if kc % 2 == 0:
            S.op('act', lambda e, pb=pb, kc=kc: e.copy(xT[:, kc, 0:nsub * 128], pb[:, 0:nsub * 128]), r=[('pbig', kc % 2)], w=[(tag + '_xT', kc)])
        else:
            S.op('dve', lambda e, pb=pb, kc=kc: e.tensor_copy(xT[:, kc, 0:nsub * 128], pb[:, 0:nsub * 128]), r=[('pbig', kc % 2)], w=[(tag + '_xT', kc)])


class _Stop(Exception):
    pass


def gdn_phase(*a, **k):
    try:
        _gdn_phase(*a, **k)
    except _Stop:
        pass


def _gdn_phase(nc, S, es, C, xin, wqkv, wz, wab, convw, hp, normw, og, nblk=T // TB):
    import os
    STOP = int(os.environ.get('STOP_AT', '99'))
    def stage(n):
        if STOP == n:
            raise _Stop()
    S.uid += 1
    uq = "u%d_" % S.uid
    sb = lambda name, shape, dt: es.enter_context(nc.sbuf_tensor(uq + name, shape, dt))
    ps = lambda name, shape, dt: es.enter_context(nc.psum_tensor(uq + name, shape, dt))
    idf, idb = C['idf'], C['idb']
    wqkv_b = sb("g_wqkv", [128, 8, 1536], BF16)
    wz_b = sb("g_wz", [128, 8, 512], BF16)
    wab_b = sb("g_wab", [128, 8, 8], BF16)
    convw_s = sb("g_convw", [128, 12, 4], F32)
    normw_s = sb("g_normw", [128, 512], F32)
    hp_s = sb("g_hp", [4, 2], F32)
    negA = sb("g_negA", [4, 1], F32)
    xt = sb("g_xt", [128, 4, 1024], F32)
    xT = sb("g_xT", [128, 8, 512], BF16)
    pc = sb("g_pc", [128, 12, 3 + TB], F32)
    cv = [sb("g_cv%d" % i, [128, TB], F32) for i in range(2)]
    qkvT = sb("g_qkvT", [128, 12, TB], F32)
    qflat = qkvT[:].rearrange("p a b -> p (a b)")
    wstage = [qflat[:, 0:768], qflat[:, 1024:1792]]
    sq = cv[1]
    zs = cv[0]
    rn = sb("g_rn", [128, TB], F32)
    qT_b = sb("g_qTb", [128, 4, TB], BF16)
    kT_b = sb("g_kTb", [128, 4, TB], BF16)
    rows = {n: sb("g_row_" + n, [4, TB], F32) for n in ('ax', 'relu', 'gc', 'ngc', 'beta', 'kbe', 'kdec')}
    for a_ in ('ex', 'ln', 'g'):
        rows[a_] = rows['ax']
    rows['egc'] = rows['kbe']
    gc_bc = sb("g_gcbc", [128, 4, TB], F32)
    egc_bc = sb("g_egcbc", [128, 4, TB], F32)
    beta_bc = sb("g_betabc", [128, 4, TB], F32)
    cols = sb("g_cols", [128, 4, 4, 4], F32)
    kbe = sb("g_kbe", [128, 4, 4, 128], BF16)
    kdec = sb("g_kdec", [128, 4, 4, 128], BF16)
    bv = sb("g_bv", [128, 4, 4, 128], BF16)
    zg = sb("g_zg", [128, 4, 512], F32)
    S32 = sb("g_S32", [128, 4, 128], F32)
    S16 = sb("g_S16", [128, 4, 128], BF16)
    NP = 2
    arg = [sb("g_arg%d" % i, [128, 128], F32) for i in range(NP)]
    E1 = [sb("g_E1%d" % i, [128, 128], F32) for i in range(NP)]
    G = [sb("g_G%d" % i, [128, 128], F32) for i in range(NP)]
    Gs = [sb("g_Gs%d" % i, [128, 128], F32) for i in range(NP)]
    Gb = [sb("g_Gb%d" % i, [128, 128], F32) for i in range(NP)]
    XY = [[sb("g_XY%d_%d" % (i, j), [128, 256], F32) for j in range(2)] for i in range(NP)]
    Mt = [[sb("g_M%d_%d" % (i, j), [128, 128], F32) for j in range(2)] for i in range(NP)]
    AiT = [sb("g_AiT%d" % i, [128, 128], BF16) for i in range(NP)]
    attnT = [sb("g_attnT%d" % i, [128, 128], BF16) for i in range(NP)]
    nwT = [sb("g_nwT%d" % i, [128, 128], BF16) for i in range(NP)]
    qdT = [sb("g_qdT%d" % i, [128, 128], BF16) for i in range(NP)]
    vn_b = [sb("g_vnb%d" % i, [128, 128], BF16) for i in range(NP)]
    junk = [sb("g_junk%d" % i, [128, 128], F32) for i in range(NP)]
    ss = [sb("g_ss%d" % i, [128, 4], F32) for i in range(NP)]
    ogt = [sb("g_ogt%d" % i, [128, 512], BF16) for i in range(2)]
    pbig = [ps("g_pbig%d" % i, [128, 512], F32) for i in range(2)]
    psm = ps("g_psm", [128, 512], F32)
    pkq = ps("g_pkq", [128, 512], F32)
    pinv = [ps("g_pinv%d" % i, [128, 512], F32) for i in range(2)]
    pscan = [ps("g_pscan%d" % i, [128, 512], F32) for i in range(2)]
    ptb = pscan
    load_cast_weight(nc, S, wstage, wqkv, wqkv_b, 1536, 'g_wqkv')
    load_cast_weight(nc, S, wstage, wz, wz_b, 512, 'g_wz')
    load_cast_weight(nc, S, wstage, wab, wab_b, 8, 'g_wab')
    S.dma('sp', convw_s[:], convw, w=['g_convw'])
    S.dma('sp', normw_s[:], normw, w=['g_normw'])
    S.dma('sp', hp_s[:], hp, w=['g_hp'])
    S.op('act', lambda e: e.activation(negA[:], hp_s[:, 1:2], AF.Exp), r=['g_hp'], w=['g_negA'])
    S.op('dve', lambda e: e.tensor_scalar(negA[:], negA[:], -1.0, None, ALU.mult), r=['g_negA'], w=['g_negA'])
    S.op('pool', lambda e: e.memset(pc[:, :, 0:3], 0.0), w=['g_pc_halo'])
    S.op('pool', lambda e: e.memset(S32[:], 0.0), w=[('g_S32', h) for h in range(4)])
    S.op('pool', lambda e: e.memset(S16[:], 0.0), w=[('g_S16', h) for h in range(4)])
    WQ = [('g_wqkv', kc) for kc in range(8)]
    WZ = [('g_wz', kc) for kc in range(8)]
    WAB = [('g_wab', kc) for kc in range(8)]
    XT = [('g_xT', kc) for kc in range(8)]

    stage(0)
    it = 0
    for blk in range(nblk):
        t0 = blk * TB
        load_xT(nc, S, C, xin[t0:t0 + TB, :], xt, xT, pbig, 'g')
        stage(1)
        for half in (0, 1):
            for kc in range(8):
                S.op('pe', lambda e, kc=kc, half=half: e.matmul(pbig[half][0:4, 0:TB], wab_b[:, kc, half * 4:half * 4 + 4], xT[:, kc, :], start=(kc == 0), stop=(kc == 7)),
                     r=[('g_wab', kc), ('g_xT', kc)], w=[('pbig', half)], sig=(kc == 7))
        R = rows
        S.op('act', lambda e: e.activation(R['beta'][:], pbig[1][0:4, 0:TB], AF.Sigmoid), r=[('pbig', 1)], w=['r_beta'])
        S.op('act', lambda e: e.activation(R['ax'][:], pbig[0][0:4, 0:TB], AF.Abs, bias=hp_s[:, 0:1]), r=[('pbig', 0), 'g_hp'], w=['r_ax'])
        S.op('act', lambda e: e.activation(R['relu'][:], pbig[0][0:4, 0:TB], AF.Relu, bias=hp_s[:, 0:1]), r=[('pbig', 0), 'g_hp'], w=['r_relu'])
        S.op('act', lambda e: e.activation(R['ex'][:], R['ax'][:], AF.Exp, scale=-1.0), r=['r_ax'], w=['r_ax'])
        S.op('act', lambda e: e.activation(R['ln'][:], R['ex'][:], AF.Ln, bias=1.0), r=['r_ax'], w=['r_ax'])
        S.op('dve', lambda e: e.tensor_tensor(R['g'][:], R['ln'][:], R['relu'][:], ALU.add), r=['r_ax', 'r_relu'], w=['r_ax'])
        S.op('dve', lambda e: e.tensor_scalar(R['g'][:], R['g'][:], negA[:, 0:1], None, ALU.mult), r=['r_ax', 'g_negA'], w=['r_ax'])
        S.op('dve', lambda e: e.tensor_tensor_scan(R['gc'][:], C['rmask'][:].rearrange("p a b -> p (a b)"), R['g'][:], 0.0, ALU.mult, ALU.add), r=['r_ax', 'c_rmask'], w=['r_gc'])
        S.op('dve', lambda e: e.tensor_scalar(R['ngc'][:], R['gc'][:], -1.0, None, ALU.mult), r=['r_gc'], w=['r_ngc'])
        S.op('act', lambda e: e.activation(R['egc'][:], R['gc'][:], AF.Exp), r=['r_gc'], w=['r_kbe'])
        S.op('dve', lambda e: e.tensor_tensor(R['kbe'][:], R['egc'][:], R['beta'][:], ALU.mult), r=['r_kbe', 'r_beta'], w=['r_kbe'])
        for c in range(4):
            cc = slice(c * CH, (c + 1) * CH)
            S.op('act', lambda e, cc=cc, c=c: e.activation(R['kdec'][:, cc], R['gc'][:, cc], AF.Exp, bias=R['gc'][:, c * CH + CH - 1:c * CH + CH], scale=-1.0),
                 r=['r_gc'], w=[('r_kdec', c)])
        stage(2)
        for qi, qn in enumerate(('ngc', 'kbe', 'kdec', 'beta')):
            for c in range(4):
                S.op('pe', lambda e, qi=qi, qn=qn, c=c: e.transpose(psm[:, (qi * 4 + c) * 4:(qi * 4 + c) * 4 + 4], R[qn][:, c * CH:(c + 1) * CH], idf[0:4, 0:4]),
                     r=([('r_kdec', c)] if qn == 'kdec' else ['r_' + qn]) + ['c_idf'], w=['ps_cols'], sig=(qi == 3 and c == 3))
        S.op('dve', lambda e: e.tensor_copy(cols[:].rearrange("p a b c -> p (a b c)"), psm[:, 0:64]), r=['ps_cols'], w=['g_cols'])
        stage(3)
        for h in range(4):
            S.op('pe', lambda e, h=h: e.matmul(pbig[0][:, :], C['esel'][:, h, :], R['gc'][:], start=True, stop=True), r=['r_gc', 'c_esel'], w=[('pbig', 0)])
            S.op('act', lambda e, h=h: e.copy(gc_bc[:, h, :], pbig[0][:, :]), r=[('pbig', 0)], w=[('g_gcbc', h)])
            S.op('act', lambda e, h=h: e.activation(egc_bc[:, h, :], pbig[0][:, :], AF.Exp), r=[('pbig', 0)], w=[('g_egcbc', h)])
            S.op('pe', lambda e, h=h: e.matmul(pbig[1][:, :], C['esel'][:, h, :], R['beta'][:], start=True, stop=True), r=['r_beta', 'c_esel'], w=[('pbig', 1)])
            S.op('dve', lambda e, h=h: e.tensor_copy(beta_bc[:, h, :], pbig[1][:, :]), r=[('pbig', 1)], w=[('g_betabc', h)])
        stage(4)
        for ct in range(12):
            pb = pbig[ct % 2]
            for kc in range(8):
                S.op('pe', lambda e, pb=pb, ct=ct, kc=kc: e.matmul(pb[:, :], wqkv_b[:, kc, ct * 128:(ct + 1) * 128], xT[:, kc, :], start=(kc == 0), stop=(kc == 7)),
                     r=[('g_wqkv', kc), ('g_xT', kc)], w=[('pbig', ct % 2)], sig=(kc == 7))
            S.op('act', lambda e, pb=pb, ct=ct: e.copy(pc[:, ct, 3:3 + TB], pb[:, :]), r=[('pbig', ct % 2)], w=[('g_pc', ct)])
            a, b = cv[0], cv[1]
            rr = [('g_pc', ct), 'g_pc_halo', 'g_convw']
            S.op('dve', lambda e, ct=ct: e.tensor_scalar(a[:], pc[:, ct, 0:TB], convw_s[:, ct, 0:1], None, ALU.mult), r=rr, w=['g_cv0'])
            S.op('dve', lambda e, ct=ct: e.scalar_tensor_tensor(b[:], pc[:, ct, 1:1 + TB], convw_s[:, ct, 1:2], a[:], ALU.mult, ALU.add), r=rr + ['g_cv0'], w=['g_cv1'])
            S.op('dve', lambda e, ct=ct: e.scalar_tensor_tensor(a[:], pc[:, ct, 2:2 + TB], convw_s[:, ct, 2:3], b[:], ALU.mult, ALU.add), r=rr + ['g_cv1'], w=['g_cv0'])
            S.op('dve', lambda e, ct=ct: e.scalar_tensor_tensor(b[:], pc[:, ct, 3:3 + TB], convw_s[:, ct, 3:4], a[:], ALU.mult, ALU.add), r=rr + ['g_cv0'], w=['g_cv1'])
            S.op('act', lambda e, ct=ct: e.activation(qkvT[:, ct, :], b[:], AF.Silu), r=['g_cv1'], w=[('g_qkvT', ct)])
        S.op('pool', lambda e: e.tensor_copy(pc[:, :, 0:3], pc[:, :, TB:TB + 3]), r=[('g_pc', ct) for ct in range(12)], w=['g_pc_halo'])
        stage(5)
        for ct in range(8):
            pb = pbig[ct % 2]
            S.op('pool', lambda e, ct=ct: e.tensor_tensor(sq[:], qkvT[:, ct, :], qkvT[:, ct, :], ALU.mult), r=[('g_qkvT', ct)], w=['g_cv1'])
            S.op('pe', lambda e, pb=pb: e.matmul(pb[:, :], C['ones'][:, 0:128], sq[:], start=True, stop=True), r=['g_cv1', 'c_ones'], w=[('pbig', ct % 2)])
            S.op('act', lambda e, pb=pb: e.activation(rn[:], pb[:, :], AF.Sqrt, bias=RMS_EPS), r=[('pbig', ct % 2)], w=['g_rn'])
            S.op('dve', lambda e: e.reciprocal(rn[:], rn[:]), r=['g_rn'], w=['g_rn'])
            dst = qT_b if ct < 4 else kT_b
            scl = (128.0 ** -0.5) if ct < 4 else 1.0
            S.op('dve', lambda e, ct=ct, dst=dst, scl=scl: e.scalar_tensor_tensor(dst[:, ct % 4, :], qkvT[:, ct, :], scl, rn[:], ALU.mult, ALU.mult),
                 r=[('g_qkvT', ct), 'g_rn'], w=[('g_qkT', ct)])
        stage(6)
        for c in range(4):
            cc = slice(c * CH, (c + 1) * CH)
            for h in range(4):
                pb = pbig[h % 2]
                S.op('pool', lambda e, h=h, cc=cc: e.tensor_copy(sq[:, 0:128], kT_b[:, h, cc]), r=[('g_qkT', 4 + h)], w=['g_cv1'])
                S.op('pe', lambda e, pb=pb: e.transpose(pb[:, 0:128], sq[:, 0:128], idf[:]), r=['g_cv1', 'c_idf'], w=[('pbig', h % 2)], sig=False)
                S.op('pe', lambda e, pb=pb, h=h, cc=cc: e.transpose(pb[:, 128:256], qkvT[:, 8 + h, cc], idf[:]), r=[('g_qkvT', 8 + h), 'c_idf'], w=[('pbig', h % 2)])
                S.op('dve', lambda e, pb=pb, c=c, h=h: e.tensor_scalar(kbe[:, c, h, :], pb[:, 0:128], cols[:, 1, c, h:h + 1], None, ALU.mult), r=[('pbig', h % 2), 'g_cols'], w=[('g_kbe', c, h)])
                S.op('dve', lambda e, pb=pb, c=c, h=h: e.tensor_scalar(kdec[:, c, h, :], pb[:, 0:128], cols[:, 2, c, h:h + 1], None, ALU.mult), r=[('pbig', h % 2), 'g_cols'], w=[('g_kdec', c, h)])
                S.op('dve', lambda e, pb=pb, c=c, h=h: e.tensor_scalar(bv[:, c, h, :], pb[:, 128:256], cols[:, 3, c, h:h + 1], None, ALU.mult), r=[('pbig', h % 2), 'g_cols'], w=[('g_bv', c, h)])
            pb = pbig[c % 2]
            for kc in range(8):
                S.op('pe', lambda e, pb=pb, kc=kc, cc=cc: e.matmul(pb[:, :], xT[:, kc, cc], wz_b[:, kc, :], start=(kc == 0), stop=(kc == 7)),
                     r=[('g_wz', kc), ('g_xT', kc)], w=[('pbig', c % 2)], sig=(kc == 7))
            S.op('act', lambda e, pb=pb: e.activation(zs[:], pb[:, :], AF.Silu), r=[('pbig', c % 2)], w=['g_cv0'])
            S.op('pool', lambda e, c=c: e.tensor_tensor(zg[:, c, :], zs[:], normw_s[:], ALU.mult), r=['g_cv0', 'g_normw'], w=[('g_zg', c)])
        stage(7)
        for c in range(4):
            cc = slice(c * CH, (c + 1) * CH)
            og_t = ogt[c % 2]
            for h in range(4):
                p = it % NP
                it += 1
                k_ = lambda s: (s, p)
                S.op('dve', lambda e, h=h, cc=cc, c=c: e.tensor_scalar(arg[p][:], gc_bc[:, h, cc], cols[:, 0, c, h:h + 1], 0.0, ALU.add, ALU.min),
                     r=[('g_gcbc', h), 'g_cols'], w=[k_('arg')])
                S.op('act', lambda e: e.activation(E1[p][:], arg[p][:], AF.Exp), r=[k_('arg')], w=[k_('E1')])
                S.op('pool', lambda e: e.tensor_tensor(G[p][:], E1[p][:], C['mincl'][:], ALU.mult), r=[k_('E1'), 'c_mincl'], w=[k_('G')])
                S.op('pool', lambda e: e.tensor_tensor(Gs[p][:], E1[p][:], C['mstrict'][:], ALU.mult), r=[k_('E1'), 'c_mstrict'], w=[k_('Gs')])
                S.op('pool', lambda e, h=h, cc=cc: e.tensor_tensor(Gb[p][:], Gs[p][:], beta_bc[:, h, cc], ALU.mult), r=[k_('Gs'), ('g_betabc', h)], w=[k_('Gb')])
                S.op('pe', lambda e, h=h, cc=cc: e.matmul(pkq[:, 0:128], kT_b[:, h, cc], kT_b[:, h, cc], start=True, stop=True), r=[('g_qkT', 4 + h)], w=['pkq'], sig=False)
                S.op('pe', lambda e, h=h, cc=cc: e.matmul(pkq[:, 128:256], kT_b[:, h, cc], qT_b[:, h, cc], start=True, stop=True), r=[('g_qkT', 4 + h), ('g_qkT', h)], w=['pkq'])
                X0 = XY[p][0]
                S.op('dve', lambda e: e.tensor_tensor(X0[:, 0:128], pkq[:, 0:128], Gb[p][:], ALU.mult), r=['pkq', k_('Gb')], w=[k_('XY0')])
                S.op('dve', lambda e: e.tensor_tensor(attnT[p][:], pkq[:, 128:256], G[p][:], ALU.mult), r=['pkq', k_('G')], w=[k_('attnT')])
                S.op('pool', lambda e, h=h, cc=cc: e.tensor_tensor(qdT[p][:], qT_b[:, h, cc], egc_bc[:, h, cc], ALU.mult), r=[('g_qkT', h), ('g_egcbc', h)], w=[k_('qdT')])
                stage(8)
                pi = pinv[p]
                S.op('pe', lambda e: e.transpose(pi[:, 128:256], X0[:, 0:128], idf[:]), r=[k_('XY0'), 'c_idf'], w=[k_('pinv')])
                S.op('act', lambda e: e.copy(X0[:, 128:256], pi[:, 128:256]), r=[k_('pinv')], w=[k_('XY0y')])
                S.op('dve', lambda e: e.tensor_tensor(Mt[p][0][:], idf[:], X0[:, 0:128], ALU.subtract), r=[k_('XY0'), 'c_idf'], w=[k_('M0')])
                stage(85)
                for lv in range(1, NLEV + 2):
                    stage(85 + lv)
                    src = XY[p][(lv - 1) % 2]
                    dst = XY[p][lv % 2]
                    msrc = Mt[p][lv % 2]
                    mdst = Mt[p][(lv + 1) % 2]
                    sk = [k_('XY%d' % ((lv - 1) % 2)), k_('XY%dy' % ((lv - 1) % 2))]
                    mk = k_('M%d' % (lv % 2))
                    pe_ops = []
                    if lv <= NLEV:
                        pe_ops.append((lambda e, src=src: e.matmul(pi[:, 0:128], src[:, 128:256], src[:, 0:128], start=True, stop=True), sk))
                        pe_ops.append((lambda e, src=src: e.matmul(pi[:, 128:256], src[:, 0:128], src[:, 128:256], start=True, stop=True), sk))
                    if lv >= 2:
                        pe_ops.append((lambda e, msrc=msrc, src=src: e.matmul(pi[:, 256:384], src[:, 128:256], msrc[:], start=True, stop=True), sk + [mk]))
                    for i_, (fn_, rr_) in enumerate(pe_ops):
                        S.op('pe', fn_, r=rr_, w=[k_('pinv')], sig=(i_ == len(pe_ops) - 1))
                    if lv <= NLEV:
                        wk = [k_('XY%d' % (lv % 2)), k_('XY%dy' % (lv % 2))]
                        S.op('dve', lambda e, dst=dst: e.tensor_copy(dst[:, 0:256], pi[:, 0:256]), r=[k_('pinv')], w=wk)
                    if lv >= 2:
                        if lv <= NLEV:
                            S.op('dve', lambda e, mdst=mdst, msrc=msrc: e.tensor_tensor(mdst[:], pi[:, 256:384], msrc[:], ALU.add), r=[k_('pinv'), mk], w=[k_('M%d' % ((lv + 1) % 2))])
                        else:
                            S.op('dve', lambda e, msrc=msrc: e.tensor_tensor(AiT[p][:], pi[:, 256:384], msrc[:], ALU.add), r=[k_('pinv'), mk], w=[k_('AiT')])
                stage(9)
                S.op('pe', lambda e, c=c, h=h: e.matmul(pkq[:, 256:384], kbe[:, c, h, :], AiT[p][:], start=True, stop=True), r=[('g_kbe', c, h), k_('AiT')], w=['pkq'])
                S.op('act', lambda e: e.activation(nwT[p][:], pkq[:, 256:384], AF.Copy, scale=-1.0), r=['pkq'], w=[k_('nwT')])
                stage(10)
                pscn = pscan[p]
                S.op('pe', lambda e, c=c, h=h: e.matmul(pscn[:, 0:128], AiT[p][:], bv[:, c, h, :], start=True, stop=False), r=[k_('AiT'), ('g_bv', c, h)], w=[k_('pscan')], sig=False)
                S.op('pe', lambda e, h=h: e.matmul(pscn[:, 0:128], nwT[p][:], S16[:, h, :], start=False, stop=True), r=[k_('nwT'), ('g_S16', h)], w=[k_('pscan')])
                S.op('act', lambda e: e.copy(vn_b[p][:], pscn[:, 0:128]), r=[k_('pscan')], w=[k_('vnb')])
                S.op('pe', lambda e, h=h: e.matmul(pscn[:, 128:256], qdT[p][:], S16[:, h, :], start=True, stop=False), r=[k_('qdT'), ('g_S16', h)], w=[k_('pscan')], sig=False)
                S.op('pe', lambda e: e.matmul(pscn[:, 128:256], attnT[p][:], vn_b[p][:], start=False, stop=True), r=[k_('attnT'), k_('vnb')], w=[k_('pscan')])
                S.op('pe', lambda e, c=c, h=h: e.matmul(pscn[:, 256:384], kdec[:, c, h, :], vn_b[p][:], start=True, stop=True), r=[('g_kdec', c, h), k_('vnb')], w=[k_('pscan')])
                S.op('dve', lambda e, h=h, c=c: e.scalar_tensor_tensor(S32[:, h, :], S32[:, h, :], egc_bc[:, h, c * CH + CH - 1:c * CH + CH], pscn[:, 256:384], ALU.mult, ALU.add),
                     r=[k_('pscan'), ('g_S32', h), ('g_egcbc', h)], w=[('g_S32', h)])
                S.op('act', lambda e, h=h: e.copy(S16[:, h, :], S32[:, h, :]), r=[('g_S32', h)], w=[('g_S16', h)])
                stage(11)
                S.op('act', lambda e, h=h: e.activation(junk[p][:], pscn[:, 128:256], AF.Square, accum_out=ss[p][:, h:h + 1]), r=[k_('pscan')], w=[k_('junk'), (k_('ss'), h)])
                S.op('dve', lambda e, h=h: e.tensor_scalar(ss[p][:, h:h + 1], ss[p][:, h:h + 1], 1.0 / 128.0, RMS_EPS, ALU.mult, ALU.add), r=[(k_('ss'), h)], w=[(k_('ss'), h)])
                S.op('act', lambda e, h=h: e.activation(ss[p][:, h:h + 1], ss[p][:, h:h + 1], AF.Sqrt), r=[(k_('ss'), h)], w=[(k_('ss'), h)])
                S.op('dve', lambda e, h=h: e.reciprocal(ss[p][:, h:h + 1], ss[p][:, h:h + 1]), r=[(k_('ss'), h)], w=[(k_('ss'), h)])
                S.op('dve', lambda e, h=h, c=c, og_t=og_t: e.scalar_tensor_tensor(og_t[:, h * 128:(h + 1) * 128], pscn[:, 128:256], ss[p][:, h:h + 1], zg[:, c, h * 128:(h + 1) * 128], ALU.mult, ALU.mult),
                     r=[k_('pscan'), (k_('ss'), h), ('g_zg', c)], w=[('g_ogt', c % 2, h)])
            S.dma('sp', og[t0 + c * CH:t0 + (c + 1) * CH, :], og_t[:], r=[('g_ogt', c % 2, h) for h in range(4)], w=[('og', blk, c)])


ALPHA = 8.0 ** 0.25
LN_EPS = 1e-5


def oln_phase(nc, S, es, C, og_parts, xin, wout, lng, lnb, xout, t_start=0, t_end=8192, tag='o', og_all=None, idx_dram=None):
    S.uid += 1
    uq = "u%d_" % S.uid
    sb = lambda name, shape, dt: es.enter_context(nc.sbuf_tensor(uq + tag + name, shape, dt))
    ps = lambda name, shape, dt: es.enter_context(nc.psum_tensor(uq + tag + name, shape, dt))
    idb = C['idb']
    wstage = [sb("_wst%d" % i, [128, 768], F32) for i in range(2)]
    wout_b = sb("_wout", [128, 8, 1024], BF16)
    lng_s = sb("_lng", [128, 1024], F32)
    lnb_s = sb("_lnb", [128, 1024], F32)
    ot = [sb("_ot%d" % i, [128, 1024], BF16) for i in range(2)]
    oT = [sb("_oT%d" % i, [128, 8, 128], BF16) for i in range(2)]
    xt = [sb("_xt%d" % i, [128, 1024], F32) for i in range(2)]
    rr = [sb("_r%d" % i, [128, 1024], F32) for i in range(2)]
    xc = sb("_xc", [128, 1024], F32)
    junk = sb("_junk", [128, 1024], F32)
    st = [sb("_st%d" % i, [128, 4], F32) for i in range(2)]
    yo = [sb("_yo%d" % i, [128, 1024], F32) for i in range(2)]
    pT = ps("_pT", [128, 1024], BF16)
    py = [[ps("_py%d_%d" % (i, j), [128, 512], F32) for j in range(2)] for i in range(2)]
    wv = wout.rearrange("(kc p) n -> p kc n", p=128)
    i = 0
    for kc in range(8):
        for c0 in (0, 512):
            stg = wstage[i % 2]
            S.dma('sp', stg[:, 0:512], wv[:, kc, c0:c0 + 512], w=[(tag + 'wst', i % 2)])
            S.op('pool' if i % 2 == 0 else 'dve', lambda e, stg=stg, kc=kc, c0=c0: e.tensor_copy(wout_b[:, kc, c0:c0 + 512], stg[:, 0:512]), r=[(tag + 'wst', i % 2)], w=[(tag + 'wout', kc)])
            i += 1
    if og_all is not None:
        idx_s = sb("_idx", [128, 2, 64], mybir.dt.int32)
        S.dma('sp', idx_s[:], idx_dram, w=[tag + 'idx'])
    S.dma('sp', lng_s[:], lng, w=[tag + 'lng'])
    S.dma('sp', lnb_s[:], lnb, w=[tag + 'lnb'])
    WO = [(tag + 'wout', kc) for kc in range(8)]
    for ti, tok in enumerate(range(t_start, t_end, 128)):
        p = ti % 2
        K = lambda s: (tag + s, p)
        if og_all is None:
            S.dma('sp', ot[p][:, 0:512], og_parts[0][tok:tok + 128, :], w=[K('ot')])
            S.dma('sp', ot[p][:, 512:1024], og_parts[1][tok:tok + 128, :], w=[K('ot')])
        else:
            for part in range(2):
                S.dma_fn('pool', lambda e, part=part, ti=ti: e.indirect_dma_start(
                    out=ot[p][:, part * 512:(part + 1) * 512], out_offset=None, in_=og_all[:, :],
                    in_offset=bass.IndirectOffsetOnAxis(ap=idx_s[:, part, tok // 128:tok // 128 + 1], axis=0)),
                    r=[tag + 'idx'], w=[K('ot')])
        S.dma('sp', xt[p][:], xin[tok:tok + 128, :], w=[K('xt')])
        for kc in range(8):
            S.op('pe', lambda e, kc=kc: e.transpose(pT[:, kc * 128:(kc + 1) * 128], ot[p][:, kc * 128:(kc + 1) * 128], idb[:]), r=[K('ot'), 'c_idb'], w=['p' + tag + 'T'], sig=(kc == 7))
        S.op('act', lambda e: e.copy(oT[p][:].rearrange("p a b -> p (a b)"), pT[:, :]), r=['p' + tag + 'T'], w=[K('oT')])
        for half in range(2):
            for kc in range(8):
                S.op('pe', lambda e, kc=kc, half=half: e.matmul(py[p][half][:, :], oT[p][:, kc, :], wout_b[:, kc, half * 512:(half + 1) * 512], start=(kc == 0), stop=(kc == 7)),
                     r=[K('oT'), (tag + 'wout', kc)], w=[('p' + tag + 'y', p, half)], sig=(kc == 7))
            S.op('dve', lambda e, half=half: e.scalar_tensor_tensor(rr[p][:, half * 512:(half + 1) * 512], xt[p][:, half * 512:(half + 1) * 512], ALPHA, py[p][half][:, :], ALU.mult, ALU.add),
                 r=[K('xt'), ('p' + tag + 'y', p, half)], w=[K('r')])
        s_ = st[p]
        S.op('dve', lambda e: e.reduce_sum(s_[:, 0:1], rr[p][:], AX.X), r=[K('r')], w=[K('st')])
        S.op('dve', lambda e: e.tensor_scalar(s_[:, 0:1], s_[:, 0:1], -1.0 / 1024.0, None, ALU.mult), r=[K('st')], w=[K('st')])
        S.op('pool', lambda e: e.tensor_scalar(xc[:], rr[p][:], s_[:, 0:1], None, ALU.add), r=[K('r'), K('st')], w=[tag + 'xc'])
        S.op('act', lambda e: e.activation(junk[:], xc[:], AF.Square, accum_out=s_[:, 1:2]), r=[tag + 'xc'], w=[tag + 'junk', K('st')])
        S.op('dve', lambda e: e.tensor_scalar(s_[:, 1:2], s_[:, 1:2], 1.0 / 1024.0, LN_EPS, ALU.mult, ALU.add), r=[K('st')], w=[K('st')])
        S.op('act', lambda e: e.activation(s_[:, 1:2], s_[:, 1:2], AF.Sqrt), r=[K('st')], w=[K('st')])
        S.op('dve', lambda e: e.reciprocal(s_[:, 1:2], s_[:, 1:2]), r=[K('st')], w=[K('st')])
        S.op('dve', lambda e: e.scalar_tensor_tensor(yo[p][:], xc[:], s_[:, 1:2], lng_s[:], ALU.mult, ALU.mult), r=[tag + 'xc', K('st'), tag + 'lng'], w=[K('yo')])
        S.op('pool', lambda e: e.tensor_tensor(yo[p][:], yo[p][:], lnb_s[:], ALU.add), r=[K('yo'), tag + 'lnb'], w=[K('yo')])
        S.dma('sp', xout[tok:tok + 128, :], yo[p][:], r=[K('yo')], w=[(tag + 'xout', ti)])


def load_xT(nc, S, C, xdram_blk, xt, xT, pbig, tag, nsub=4, F=1024, xT32=None, xms=None, xms_col=None):
    S.dma('sp', xt[:, 0:nsub, :], xdram_blk.rearrange("(s p) f -> p s f", p=128), w=[tag + '_xt'])
    nk = F // 128
    for kc in range(nk):
        pb = pbig[kc % 2]
        pk = ('pbig', kc % 2)
        for s in range(nsub):
            S.op('pe', lambda e, pb=pb, s=s, kc=kc: e.transpose(pb[:, s * 128:(s + 1) * 128], xt[:, s, kc * 128:(kc + 1) * 128], C['idf'][:]),
                 r=[tag + '_xt', 'c_idf'], w=[pk], sig=(s == nsub - 1))
        if kc % 2 == 0:
            S.op('act', lambda e, pb=pb, kc=kc: e.copy(xT[:, kc, 0:nsub * 128], pb[:, 0:nsub * 128]), r=[pk], w=[(tag + '_xT', kc)])
        else:
            S.op('dve', lambda e, pb=pb, kc=kc: e.tensor_copy(xT[:, kc, 0:nsub * 128], pb[:, 0:nsub * 128]), r=[pk], w=[(tag + '_xT', kc)])
        if xT32 is not None:
            S.op('dve', lambda e, pb=pb, kc=kc: e.tensor_copy(xT32[:, kc, 0:nsub * 128], pb[:, 0:nsub * 128]), r=[pk], w=[(tag + '_xT32', kc)])
        if xms is not None:
            S.op('dve', lambda e, pb=pb, kc=kc: e.reduce_sum(xms[:, kc, xms_col:xms_col + nsub // 2], pb[:, 0:nsub * 128].rearrange("p (a b) -> p a b", b=256), AX.X), r=[pk], w=[(tag + '_xms', kc)])

T = 8192
BLK = 256
NB = T // BLK
BIG = 30000.0
QSCALE = 128.0 ** -0.5


def _load_w(nc, S, stage, wdram, wsb, name, skey):
    wv = wdram.rearrange("(kc p) n -> p kc n", p=128)
    for kc in range(8):
        st = stage[kc % 2]
        S.dma('sp', st[:, :], wv[:, kc, :], w=[(skey, kc % 2)])
        S.op('dve', lambda e, st=st, kc=kc: e.tensor_copy(wsb[:, kc, :], st[:, :]), r=[(skey, kc % 2)], w=[(name, kc)])


def moba_phase(nc, S, es, C, xkv, xq, wk, wv, wq, wz, og, nqb=NB, nkvb=NB):
    S.uid += 1
    uq = "u%d_m_" % S.uid
    sb = lambda name, shape, dt: es.enter_context(nc.sbuf_tensor(uq + name, shape, dt))
    ps = lambda name, shape, dt: es.enter_context(nc.psum_tensor(uq + name, shape, dt))
    idf, idb = C['idf'], C['idb']
    KT = sb("KT", [128, 4, T], BF16)
    VA = sb("VA", [128, T // 128, 4, 129], BF16)
    xms = sb("xms", [128, 8, NB], F32)
    kmT32 = sb("kmT32", [128, 4, NB], F32)
    G32 = sb("G32", [128, 8, 128], F32)
    xT32 = sb("xT32", [128, 8, BLK], F32)
    kmax2 = sb("kmax2", [128, 4], F32)
    xt = sb("xt", [128, 2, 1024], F32)
    xT = sb("xT", [128, 8, BLK], BF16)
    wst = [sb("wst%d" % i, [128, 512], F32) for i in range(2)]
    mincl_b = sb("minclb", [128, 128], BF16)
    pbig = [ps("pbig%d" % i, [128, 512], F32) for i in range(2)]
    S.op('pool', lambda e: e.tensor_copy(mincl_b[:], C['mincl'][:]), r=['c_mincl'], w=['m_minclb'])
    S.op('pool', lambda e: e.memset(VA[:, :, :, 128:129], 1.0), w=['m_VAones'])
    S.op('pool', lambda e: e.memset(kmax2[:], 0.0), w=['m_kmax2'])
    S.op('pool', lambda e: e.memset(xms[:], 0.0), w=[('m_xms', kc) for kc in range(8)])
    with ExitStack() as es2:
        sb2 = lambda name, shape, dt: es2.enter_context(nc.sbuf_tensor(uq + name, shape, dt))
        wk_b = sb2("wk", [128, 8, 512], BF16)
        wv_b = sb2("wv", [128, 8, 512], BF16)
        ksq = sb2("ksq", [128, BLK], BF16)
        mx = sb2("mx", [128, 2], F32)
        _load_w(nc, S, wst, wk, wk_b, 'm_wk', 'm_wst')
        _load_w(nc, S, wst, wv, wv_b, 'm_wv', 'm_wst')
        for blk in range(nkvb):
            t0 = blk * BLK
            load_xT(nc, S, C, xkv[t0:t0 + BLK, :], xt, xT, pbig, 'm', nsub=2, xms=xms, xms_col=blk)
            for h in range(4):
                pb = pbig[h % 2]
                pk = ('pbig', h % 2)
                for kc in range(8):
                    S.op('pe', lambda e, pb=pb, kc=kc, h=h: e.matmul(pb[:, 0:BLK], wk_b[:, kc, h * 128:(h + 1) * 128], xT[:, kc, :], start=(kc == 0), stop=(kc == 7)),
                         r=[('m_wk', kc), ('m_xT', kc)], w=[pk], sig=(kc == 7))
                S.op('act', lambda e, pb=pb, h=h: e.copy(KT[:, h, t0:t0 + BLK], pb[:, 0:BLK]), r=[pk], w=[('m_KT', h, blk)])
                S.op('dve', lambda e, h=h: e.tensor_tensor(ksq[:], KT[:, h, t0:t0 + BLK], KT[:, h, t0:t0 + BLK], ALU.mult), r=[('m_KT', h, blk)], w=['m_ksq'])
                S.op('pe', lambda e, pb=pb: e.matmul(pb[:, 0:BLK], C['onesb'][:], ksq[:], start=True, stop=True), r=['m_ksq', 'c_onesb'], w=[pk])
                S.op('dve', lambda e, pb=pb: e.reduce_max(mx[:, 0:1], pb[:, 0:BLK], AX.X), r=[pk], w=['m_mx'])
                S.op('dve', lambda e, h=h: e.tensor_tensor(kmax2[:, h:h + 1], kmax2[:, h:h + 1], mx[:, 0:1], ALU.max), r=['m_mx', 'm_kmax2'], w=['m_kmax2'])
            for s in range(2):
                pb = pbig[s % 2]
                pk = ('pbig', s % 2)
                for kc in range(8):
                    S.op('pe', lambda e, pb=pb, kc=kc, s=s: e.matmul(pb[:, :], xT[:, kc, s * 128:(s + 1) * 128], wv_b[:, kc, :], start=(kc == 0), stop=(kc == 7)),
                         r=[('m_wv', kc), ('m_xT', kc)], w=[pk], sig=(kc == 7))
                ti = blk * 2 + s
                S.op('act' if s % 2 == 0 else 'dve',
                     (lambda e, pb=pb, ti=ti: e.copy(VA[:, ti, :, 0:128], pb[:, :].rearrange("p (a b) -> p a b", b=128))) if s % 2 == 0 else
                     (lambda e, pb=pb, ti=ti: e.tensor_copy(VA[:, ti, :, 0:128], pb[:, :].rearrange("p (a b) -> p a b", b=128))),
                     r=[pk], w=[('m_VA', ti)])
    S.barrier()
    with ExitStack() as es3:
        sb3 = lambda name, shape, dt: es3.enter_context(nc.sbuf_tensor(uq + name, shape, dt))
        w32 = sb3("w32", [128, 8, 512], F32)
        wqT = sb3("wqT", [128, 1024], F32)
        S.dma('sp', w32[:], wk.rearrange("(kc p) n -> p kc n", p=128), w=['m_w32'])
        for h in range(4):
            for kc in range(8):
                S.op('pe', lambda e, h=h, kc=kc: e.matmul(pbig[0][:, h * 32:(h + 1) * 32], w32[:, kc, h * 128:(h + 1) * 128], xms[:, kc, :], start=(kc == 0), stop=(kc == 7)),
                     r=['m_w32', ('m_xms', kc)], w=[('pbig', 0)], sig=(kc == 7))
        S.op('dve', lambda e: e.tensor_scalar(kmT32[:].rearrange("p a b -> p (a b)"), pbig[0][:, 0:128], 1.0 / 256.0, None, ALU.mult), r=[('pbig', 0)], w=['m_kmT32'])
        S.dma('sp', w32[:], wq.rearrange("(kc p) n -> p kc n", p=128), w=['m_w32'])
        for h in range(4):
            for kc in range(8):
                pb = pbig[kc % 2]
                S.op('pe', lambda e, pb=pb, h=h, kc=kc: e.transpose(pb[:, 0:128], w32[:, kc, h * 128:(h + 1) * 128], idf[:]), r=['m_w32', 'c_idf'], w=[('pbig', kc % 2)])
                S.op('dve', lambda e, pb=pb, kc=kc: e.tensor_copy(wqT[:, kc * 128:(kc + 1) * 128], pb[:, 0:128]), r=[('pbig', kc % 2)], w=['m_wqT'])
            for kc in range(8):
                pb = pbig[kc % 2]
                S.op('pe', lambda e, pb=pb, h=h, kc=kc: e.matmul(pb[:, 0:32], wqT[:, kc * 128:(kc + 1) * 128], kmT32[:, h, :], start=True, stop=True),
                     r=['m_wqT', 'm_kmT32'], w=[('pbig', kc % 2)])
                S.op('dve', lambda e, pb=pb, kc=kc, h=h: e.tensor_copy(G32[:, kc, h * 32:(h + 1) * 32], pb[:, 0:32]), r=[('pbig', kc % 2)], w=['m_G32'])
    S.barrier()
    wq_b = sb("wq", [128, 8, 512], BF16)
    wz_b = sb("wz", [128, 8, 512], BF16)
    EN = sb("EN", [32, 32, 128], BF16)
    qT_b = sb("qTb", [128, 4, BLK], BF16)
    qsq = sb("qsq", [128, BLK], BF16)
    zs = [sb("zs%d" % i, [128, 512], F32) for i in range(2)]
    gsb = sb("gsb", [128, 40], F32)
    top8 = sb("top8", [128, 8], F32)
    mcol = sb("mcol", [128, 4], F32)
    bTM = sb("bTM", [128, 32], F32)
    biasT = [sb("biasT%d" % i, [32, BLK], BF16) for i in range(2)]
    pT_sb = [sb("pTsb%d" % i, [128, 2 * BLK], BF16) for i in range(3)]
    rinv = sb("rinv", [128, 4], F32)
    ogt = [sb("ogt%d" % i, [128, 512], BF16) for i in range(2)]
    pst = [ps("pst%d" % i, [128, 512], F32) for i in range(2)]
    po = [ps("po%d" % i, [128, 512], F32) for i in range(2)]
    pg = ps("pg", [128, 512], F32)
    pgate = ps("pgate", [128, 512], F32)
    gate_sb = sb("gate", [128, 2, 128], F32)
    _load_w(nc, S, wst, wq, wq_b, 'm_wq', 'm_wst')
    _load_w(nc, S, wst, wz, wz_b, 'm_wz', 'm_wst')
    S.op('pool', lambda e: e.memset(EN[:], 1.0), w=['m_EN'])
    S.op('pool', lambda e: e.affine_select(EN[:], EN[:], [[-1, 32], [0, 128]], ALU.is_equal, 0.0, base=0, channel_multiplier=1), r=['m_EN'], w=['m_EN'])
    ist = 0
    for i in range(nqb):
        t0 = i * BLK
        load_xT(nc, S, C, xq[t0:t0 + BLK, :], xt, xT, pbig, 'm', nsub=2, xT32=xT32)
        for h in range(4):
            pb = pbig[h % 2]
            pk = ('pbig', h % 2)
            for kc in range(8):
                S.op('pe', lambda e, pb=pb, kc=kc, h=h: e.matmul(pb[:, 0:BLK], wq_b[:, kc, h * 128:(h + 1) * 128], xT[:, kc, 0:BLK], start=(kc == 0), stop=(kc == 7)),
                     r=[('m_wq', kc), ('m_xT', kc)], w=[pk], sig=(kc == 7))
            S.op('act', lambda e, pb=pb, h=h: e.activation(qT_b[:, h, :], pb[:, 0:BLK], AF.Copy, scale=QSCALE), r=[pk], w=[('m_qT', h)])
        for s in range(2):
            pb = pbig[s % 2]
            pk = ('pbig', s % 2)
            for kc in range(8):
                S.op('pe', lambda e, pb=pb, kc=kc, s=s: e.matmul(pb[:, :], xT[:, kc, s * 128:(s + 1) * 128], wz_b[:, kc, :], start=(kc == 0), stop=(kc == 7)),
                     r=[('m_wz', kc), ('m_xT', kc)], w=[pk], sig=(kc == 7))
            S.op('act', lambda e, pb=pb, s=s: e.activation(zs[s][:], pb[:, :], AF.Silu), r=[pk], w=[('m_zs', s)])
        for qt in range(2):
            for kc in range(8):
                S.op('pe', lambda e, qt=qt, kc=kc: e.matmul(pgate[:, qt * 128:(qt + 1) * 128], xT32[:, kc, qt * 128:(qt + 1) * 128], G32[:, kc, :], start=(kc == 0), stop=(kc == 7)),
                     r=[('m_xT32', kc), 'm_G32'], w=['pgate'], sig=(kc == 7))
        S.op('dve', lambda e: e.tensor_copy(gate_sb[:].rearrange("p a b -> p (a b)"), pgate[:, 0:256]), r=['pgate'], w=['m_gate'])
        def prologue_ops(i, h):
            hp_ = h % 2
            bT = biasT[hp_]
            ops = []
            A = ops.append
            A(lambda: S.op('dve', lambda e: e.tensor_tensor(qsq[:], qT_b[:, h, :], qT_b[:, h, :], ALU.mult), r=[('m_qT', h)], w=['m_qsq']))
            for qt in range(2):
                qs = slice(qt * 128, (qt + 1) * 128)
                A(lambda qs=qs: S.op('pe', lambda e: e.matmul(pg[:, 32:33], qsq[:, qs], C['onesb'][:, 0:1], start=True, stop=True), r=['m_qsq', 'c_onesb'], w=['pg']))
                A(lambda: S.op('pool', lambda e: e.memset(gsb[:, 0:33], -1e30), w=['m_gsb']))
                if i > 0:
                    A(lambda qt=qt: S.op('dve', lambda e: e.tensor_copy(gsb[:, 0:i], gate_sb[:, qt, h * 32:h * 32 + i]), r=['m_gate'], w=['m_gsb']))
                A(lambda: S.op('dve', lambda e: e.tensor_scalar(mcol[:, 0:1], pg[:, 32:33], kmax2[:, h:h + 1], None, ALU.mult), r=['pg', 'm_kmax2'], w=['m_mcol']))
                A(lambda: S.op('act', lambda e: e.activation(mcol[:, 0:1], mcol[:, 0:1], AF.Sqrt), r=['m_mcol'], w=['m_mcol']))
                A(lambda: S.op('dve', lambda e: e.tensor_scalar(mcol[:, 1:2], mcol[:, 0:1], -1.0, None, ALU.mult), r=['m_mcol'], w=['m_mcol']))
                A(lambda: S.op('dve', lambda e: e.tensor_scalar(mcol[:, 2:3], mcol[:, 0:1], -1.0, -BIG, ALU.mult, ALU.add), r=['m_mcol'], w=['m_mcol']))
                if i >= 4:
                    A(lambda: S.op('dve', lambda e: e.max(top8[:], gsb[:, 0:max(i, 8)]), r=['m_gsb'], w=['m_top8']))
                    A(lambda: S.op('dve', lambda e: e.tensor_scalar(bTM[:, 0:i], gsb[:, 0:i], top8[:, 2:3], BIG, ALU.is_ge, ALU.mult), r=['m_gsb', 'm_top8'], w=['m_bTM']))
                    A(lambda: S.op('dve', lambda e: e.tensor_scalar(bTM[:, 0:i], bTM[:, 0:i], mcol[:, 2:3], None, ALU.add), r=['m_bTM', 'm_mcol'], w=['m_bTM']))
                elif i > 0:
                    A(lambda: S.op('dve', lambda e: e.tensor_scalar(bTM[:, 0:i], gsb[:, 0:i], 0.0, mcol[:, 1:2], ALU.mult, ALU.add), r=['m_gsb', 'm_mcol'], w=['m_bTM']))
                A(lambda: S.op('dve', lambda e: e.tensor_copy(bTM[:, i:i + 1], mcol[:, 1:2]), r=['m_mcol'], w=['m_bTM']))
                A(lambda: S.op('pe', lambda e: e.transpose(pg[0:i + 1, 64:192], bTM[:, 0:i + 1], idf[:]), r=['m_bTM', 'c_idf'], w=['pg']))
                A(lambda qs=qs: S.op('act', lambda e: e.copy(bT[0:i + 1, qs], pg[0:i + 1, 64:192]), r=['pg'], w=[('m_biasT', hp_)]))
            return ops

        def attention(i, h, drip):
            nonlocal ist
            hp_ = h % 2
            bT = biasT[hp_]
            nblk_ = i + 1
            slots = []
            for j in range(nblk_):
                slots.append((ist % 2, ist % 3))
                ist += 1
            per = -(-len(drip) // nblk_)

            def scores(n):
                a, _ = slots[n]
                for kt in range(2):
                    Tk = n * 2 + kt
                    cs = slice(kt * BLK, (kt + 1) * BLK)
                    S.op('pe', lambda e, Tk=Tk, cs=cs: e.matmul(pst[a][:, cs], KT[:, h, Tk * 128:(Tk + 1) * 128], qT_b[:, h, :], start=True, stop=False),
                         r=[('m_KT', h, n), ('m_qT', h)], w=[('pst', a)], sig=False)
                    S.op('pe', lambda e, cs=cs: e.matmul(pst[a][:, cs], EN[0:i + 1, n, :], bT[0:i + 1, :], start=False, stop=True),
                         r=['m_EN', ('m_biasT', hp_)], w=[('pst', a)], sig=(kt == 1))

            def probs_pv(n):
                a, b3 = slots[n]
                tsb = pT_sb[b3]
                tk = ('m_pTsb', b3)
                S.op('act', lambda e: e.activation(tsb[:], pst[a][:, :], AF.Exp), r=[('pst', a)], w=[tk])
                if n == i:
                    S.op('dve', lambda e: e.tensor_tensor(tsb[:, 0:128], tsb[:, 0:128], mincl_b[:], ALU.mult), r=[tk, 'm_minclb'], w=[tk])
                    S.op('dve', lambda e: e.tensor_tensor(tsb[:, 384:512], tsb[:, 384:512], mincl_b[:], ALU.mult), r=[tk, 'm_minclb'], w=[tk])
                for kt in range(2):
                    Tk = n * 2 + kt
                    for qt in range(2):
                        if n == i and kt > qt:
                            continue
                        first = (Tk == 0)
                        last = (n == i and kt == qt)
                        c0 = kt * BLK + qt * 128
                        S.op('pe', lambda e, qt=qt, first=first, last=last, c0=c0, Tk=Tk: e.matmul(po[qt][:, 0:129], tsb[:, c0:c0 + 128], VA[:, Tk, h, :], start=first, stop=last),
                             r=[tk, ('m_VA', Tk), 'm_VAones'], w=[('po', qt)], sig=(last or (kt == 1 and qt == 1)))

            scores(0)
            for n in range(nblk_):
                if n + 1 < nblk_:
                    scores(n + 1)
                probs_pv(n)
                for _ in range(per):
                    if drip:
                        drip.pop(0)()
            while drip:
                drip.pop(0)()
            for qt in range(2):
                S.op('dve', lambda e, qt=qt: e.reciprocal(rinv[:, h:h + 1], po[qt][:, 128:129]), r=[('po', qt)], w=[('m_rinv', h)])
                S.op('dve', lambda e, qt=qt: e.scalar_tensor_tensor(ogt[qt][:, h * 128:(h + 1) * 128], po[qt][:, 0:128], rinv[:, h:h + 1], zs[qt][:, h * 128:(h + 1) * 128], ALU.mult, ALU.mult),
                     r=[('po', qt), ('m_rinv', h), ('m_zs', qt)], w=[('m_ogt', qt, h)])

        for th in prologue_ops(i, 0):
            th()
        for h in range(4):
            attention(i, h, prologue_ops(i, h + 1) if h < 3 else [])
        for qt in range(2):
            S.dma('sp', og[t0 + qt * 128:t0 + (qt + 1) * 128, :], ogt[qt][:], r=[('m_ogt', qt, h) for h in range(4)], w=[('og', i, qt)])


from concourse.bass_utils import run_bass_kernel_spmd

N_CORES = 8
I32 = mybir.dt.int32


def _dram(nc, name, shape, dt, kind):
    return nc.dram_tensor(name, shape, dt, kind=kind).ap()


def _decl_gdn(nc, pfx):
    d = {}
    d['wqkv'] = _dram(nc, pfx + "wqkv", [1024, 1536], F32, "ExternalInput")
    d['wz'] = _dram(nc, pfx + "wz", [1024, 512], F32, "ExternalInput")
    d['wab'] = _dram(nc, pfx + "wab", [1024, 8], F32, "ExternalInput")
    d['convw'] = _dram(nc, pfx + "convw", [128, 12, 4], F32, "ExternalInput")
    d['hp'] = _dram(nc, pfx + "hp", [4, 2], F32, "ExternalInput")
    d['normw'] = _dram(nc, pfx + "normw", [128, 512], F32, "ExternalInput")
    return d


def _decl_oln(nc, pfx):
    d = {}
    d['wout'] = _dram(nc, pfx + "wout", [1024, 1024], F32, "ExternalInput")
    d['lng'] = _dram(nc, pfx + "lng", [128, 1024], F32, "ExternalInput")
    d['lnb'] = _dram(nc, pfx + "lnb", [128, 1024], F32, "ExternalInput")
    return d


def _decl_moba(nc, pfx):
    return {n: _dram(nc, pfx + n, [1024, 512], F32, "ExternalInput") for n in ('wk', 'wv', 'wq', 'wz')}


def build_fused():
    nc = bass.Bass("TRN2", target_bir_lowering=False)
    xin = _dram(nc, "xin", [T, 1024], F32, "ExternalInput")
    idx = _dram(nc, "idx", [128, 2, 64], I32, "ExternalInput")
    out = _dram(nc, "out", [T, 1024], F32, "ExternalOutput")
    G = [_decl_gdn(nc, "g%d_" % i) for i in range(2)]
    O = [_decl_oln(nc, "o%d_" % i) for i in range(4)]
    M = [_decl_moba(nc, "m%d_" % i) for i in range(2)]
    og_loc = nc.dram_tensor("og_loc", [T, 512], BF16)
    og_all = nc.dram_tensor("og_all", [N_CORES * T, 512], BF16)
    xs = [nc.dram_tensor("xs%d" % i, [T, 1024], F32).ap() for i in range(3)]
    with ExitStack() as es:
        S = Sched(nc, es)
        C = setup_consts(nc, S, es)
        csems = [es.enter_context(nc.semaphore("cc%d" % i)) for i in range(4)]

        def exchange(i):
            S.new_epoch()
            nc.gpsimd.collective_compute("AllGather", ALU.bypass, replica_groups=[list(range(N_CORES))],
                                         ins=[og_loc.ap().opt()], outs=[og_all.ap().opt()]).then_inc(csems[i])
            for e in S.E.values():
                e.wait_ge(csems[i], 1)

        def oln(i, x_in, x_out):
            with ExitStack() as es1:
                oln_phase(nc, S, es1, C, None, x_in, O[i]['wout'], O[i]['lng'], O[i]['lnb'], x_out, 0, T, og_all=og_all.ap(), idx_dram=idx)
            S.new_epoch()

        def gdn(i, x_in):
            g = G[i]
            with ExitStack() as es1:
                gdn_phase(nc, S, es1, C, x_in, g['wqkv'], g['wz'], g['wab'], g['convw'], g['hp'], g['normw'], og_loc.ap())

        def moba(i, xkv, xq):
            m = M[i]
            with ExitStack() as es1:
                moba_phase(nc, S, es1, C, xkv, xq, m['wk'], m['wv'], m['wq'], m['wz'], og_loc.ap())

        S.new_epoch()
        gdn(0, xin)
        exchange(0)
        oln(0, xin, xs[0])
        gdn(1, xs[0])
        exchange(1)
        oln(1, xs[0], xs[1])
        moba(0, xs[1], xs[1])
        exchange(2)
        oln(2, xs[1], xs[2])
        moba(1, xs[1], xs[2])
        exchange(3)
        oln(3, xs[2], out)
        S.finish()
    return nc


def _c(a):
    return np.ascontiguousarray(a)


def gdn_inputs(inp, layer, hg, pfx):
    w_in = inp['a_w_in'][layer]
    hs = slice(hg * 512, (hg + 1) * 512)
    wq, wk, wv, wz_ = (w_in[:, o * 1024:(o + 1) * 1024][:, hs] for o in range(4))
    wa = w_in[:, 4096 + 8 + hg * 4: 4096 + 8 + hg * 4 + 4]
    wb = w_in[:, 4096 + hg * 4: 4096 + hg * 4 + 4]
    cw = inp['a_conv_w'][layer]
    cws = np.concatenate([cw[:, o * 1024:(o + 1) * 1024][:, hs] for o in range(3)], 1)
    d = {"wqkv": _c(np.concatenate([wq, wk, wv], 1)), "wz": _c(wz_), "wab": _c(np.concatenate([wa, wb], 1)),
         "convw": _c(cws.reshape(4, 12, 128).transpose(2, 1, 0)),
         "hp": _c(np.stack([inp['a_dt_bias'][layer][hg * 4:hg * 4 + 4], inp['a_A_log'][layer][hg * 4:hg * 4 + 4]], 1)),
         "normw": _c(np.tile(inp['a_norm_w'][layer][None, :], (128, 4)))}
    return {pfx + k: v for k, v in d.items()}


def oln_inputs(wout, g, b, pfx):
    return {pfx + "wout": _c(wout), pfx + "lng": _c(np.tile(g[None], (128, 1))), pfx + "lnb": _c(np.tile(b[None], (128, 1)))}


def moba_inputs(inp, li, hg, pfx):
    hs = slice(hg * 512, (hg + 1) * 512)
    wkv = inp['b_w_kv']
    win = inp['b_w_in'][li]
    return {pfx + "wk": _c(wkv[:, :1024][:, hs]), pfx + "wv": _c(wkv[:, 1024:][:, hs]), pfx + "wq": _c(win[:, :1024][:, hs]), pfx + "wz": _c(win[:, 1024:][:, hs])}


def kernel(**inputs):
    inp = {k: np.asarray(v) for k, v in inputs.items()}
    x = inp['x'].astype(np.float32, copy=False)
    nc = build_fused()
    in_maps = []
    for c in range(N_CORES):
        b, hg = c // 2, c % 2
        d = {"xin": _c(x[b])}
        idx = np.zeros((128, 2, 64), np.int32)
        for k in range(2):
            idx[:, k, :] = (2 * b + k) * T + np.arange(64)[None, :] * 128 + np.arange(128)[:, None]
        d["idx"] = idx
        for i in range(2):
            d.update(gdn_inputs(inp, i, hg, "g%d_" % i))
            d.update(moba_inputs(inp, i, hg, "m%d_" % i))
        d.update(oln_inputs(inp['a_w_out'][0], inp['a_ln_g'][0], inp['a_ln_b'][0], "o0_"))
        d.update(oln_inputs(inp['a_w_out'][1], inp['a_ln_g'][1], inp['a_ln_b'][1], "o1_"))
        d.update(oln_inputs(inp['b_w_out'][0], inp['b_ln_g'][0], inp['b_ln_b'][0], "o2_"))
        d.update(oln_inputs(inp['b_w_out'][1], inp['b_ln_g'][1], inp['b_ln_b'][1], "o3_"))
        in_maps.append(d)
    res = run_bass_kernel_spmd(nc, in_maps, core_ids=list(range(N_CORES)))
    out = np.empty((4, T, 1024), np.float32)
    for b in range(4):
        out[b] = res.results[2 * b]["out"]
    return out
```
